# Optimizing a Trainium2 kernel written in Bass

```python
import jax, jax.numpy as jnp
from jax import lax
import numpy as np

D_MODEL = 1024
BATCH = 1
SEQ = 16384
DEPTH = 2

HEAD_DIM = 64
MOBA_HEADS = 8
MOBA_BLOCK = 256
MOBA_TOPK = 3
SB_HEADS = 8
MLA_HEADS = 8
MLA_Q_LORA = 256
MLA_KV_LORA = 128
MLA_NOPE = 64
MLA_ROPE = 32
MLA_V = 64
MLA_QK = MLA_NOPE + MLA_ROPE
ROPE_THETA = 500000.0
PARTIAL_ROT = HEAD_DIM // 4
Q_BLOCK = 128
D_FF = 2816
N_BRANCH = 3
BRANCH_W = 512
N_SUB = 3
EPS = 1e-6

MOBA_W = MOBA_HEADS * HEAD_DIM
SB_W = SB_HEADS * HEAD_DIM
SPLIT_1 = 3 * MOBA_W
SPLIT_2 = SPLIT_1 + 3 * SB_W
SPLIT_3 = SPLIT_2 + MLA_Q_LORA
SPLIT_4 = SPLIT_3 + MLA_KV_LORA
SPLIT_5 = SPLIT_4 + MLA_ROPE
N_IN = SPLIT_5 + N_BRANCH * D_MODEL

kernel_name = 'hybrid_moba_stickbreak_mla_macaron'


def rms_norm(x, g):
    xf = x.astype(jnp.float32)
    y = xf * lax.rsqrt(jnp.mean(xf * xf, axis=-1, keepdims=True) + EPS)
    return (y * g.astype(jnp.float32)).astype(x.dtype)


def rope_tables(n_rot, seq):
    inv = jnp.power(jnp.float32(ROPE_THETA), -jnp.arange(0, n_rot, 2, dtype=jnp.float32) / n_rot)
    ang = jnp.arange(seq, dtype=jnp.float32)[:, None] * inv[None, :]
    return jnp.cos(ang), jnp.sin(ang)


def apply_rope(x, cos, sin):
    x1, x2 = jnp.split(x, 2, axis=-1)
    cos = cos.astype(x.dtype)
    sin = sin.astype(x.dtype)
    return jnp.concatenate([x1 * cos - x2 * sin, x1 * sin + x2 * cos], axis=-1)


def partial_rope(x, cos, sin):
    return jnp.concatenate([apply_rope(x[..., :PARTIAL_ROT], cos, sin), x[..., PARTIAL_ROT:]], axis=-1)


def to_heads(x, n_heads):
    b, s, _ = x.shape
    return x.reshape(b, s, n_heads, -1).transpose(0, 2, 1, 3)


def from_heads(x):
    b, h, s, d = x.shape
    return x.transpose(0, 2, 1, 3).reshape(b, s, h * d)


def query_blocks(q):
    b, h, s, d = q.shape
    nq = s // Q_BLOCK
    return q.reshape(b, h, nq, Q_BLOCK, d).transpose(2, 0, 1, 3, 4), nq


def merge_query_blocks(o):
    nq, b, h, qb, d = o.shape
    return o.transpose(1, 2, 0, 3, 4).reshape(b, h, nq * qb, d)


def causal_softmax_attention(q, k, v):
    s_len = k.shape[2]
    scale = q.shape[-1] ** -0.5
    qb, nq = query_blocks(q)
    kpos = jnp.arange(s_len)

    def one(args):
        qc, i = args
        sc = jnp.einsum('bhqd,bhkd->bhqk', qc, k).astype(jnp.float32) * scale
        qpos = i * Q_BLOCK + jnp.arange(Q_BLOCK)
        sc = jnp.where(kpos[None, :] <= qpos[:, None], sc, -jnp.inf)
        p = jax.nn.softmax(sc, axis=-1).astype(v.dtype)
        return jnp.einsum('bhqk,bhkd->bhqd', p, v)

    return merge_query_blocks(lax.map(one, (qb, jnp.arange(nq))))


def stick_breaking_attention(q, k, v):
    s_len = k.shape[2]
    scale = q.shape[-1] ** -0.5
    qb, nq = query_blocks(q)
    kpos = jnp.arange(s_len)

    def one(args):
        qc, i = args
        z = jnp.einsum('bhqd,bhkd->bhqk', qc, k).astype(jnp.float32) * scale
        qpos = i * Q_BLOCK + jnp.arange(Q_BLOCK)
        mask = kpos[None, :] < qpos[:, None]
        log_1m = jnp.where(mask, jax.nn.log_sigmoid(-z), 0.0)
        suffix = lax.cumsum(log_1m, axis=3, reverse=True) - log_1m
        w = jnp.where(mask, jnp.exp(jax.nn.log_sigmoid(z) + suffix), 0.0).astype(v.dtype)
        return jnp.einsum('bhqk,bhkd->bhqd', w, v)

    return merge_query_blocks(lax.map(one, (qb, jnp.arange(nq))))


def moba_attention(q, k, v):
    b, h, s_len, d = q.shape
    n = b * h
    scale = d ** -0.5
    s_pad = -(-s_len // MOBA_BLOCK) * MOBA_BLOCK
    nb = s_pad // MOBA_BLOCK
    top = min(MOBA_TOPK, nb)
    qf = q.reshape(n, s_len, d)
    pad = ((0, 0), (0, s_pad - s_len), (0, 0))
    kb = jnp.pad(k.reshape(n, s_len, d), pad).reshape(n, nb, MOBA_BLOCK, d)
    vb = jnp.pad(v.reshape(n, s_len, d), pad).reshape(n, nb, MOBA_BLOCK, d)
    k_mean = jnp.mean(kb.astype(jnp.float32), axis=2)
    gate = jnp.einsum('nsd,nbd->nsb', qf.astype(jnp.float32), k_mean)
    qblk = jnp.arange(s_len) // MOBA_BLOCK
    past = jnp.arange(nb)[None, :] < qblk[:, None]
    gate = jnp.where(past[None], gate, -jnp.inf)
    _, sel = lax.top_k(gate, top)
    valid = sel < qblk[None, :, None]
    nq = s_len // Q_BLOCK
    q_all = qf.reshape(n, nq, Q_BLOCK, d).transpose(1, 0, 2, 3)
    sel_all = sel.reshape(n, nq, Q_BLOCK, top).transpose(1, 0, 2, 3)
    val_all = valid.reshape(n, nq, Q_BLOCK, top).transpose(1, 0, 2, 3)
    rows = jnp.arange(n)[:, None, None]

    def one(args):
        qc, sc, vc, i = args
        ks = kb[rows, sc]
        vs = vb[rows, sc]
        own = (i * Q_BLOCK) // MOBA_BLOCK
        ko = lax.dynamic_index_in_dim(kb, own, axis=1, keepdims=False)
        vo = lax.dynamic_index_in_dim(vb, own, axis=1, keepdims=False)
        s_sel = jnp.einsum('nqd,nqkjd->nqkj', qc, ks).astype(jnp.float32) * scale
        s_sel = jnp.where(vc[..., None], s_sel, -jnp.inf).reshape(n, Q_BLOCK, top * MOBA_BLOCK)
        s_own = jnp.einsum('nqd,njd->nqj', qc, ko).astype(jnp.float32) * scale
        qpos = i * Q_BLOCK + jnp.arange(Q_BLOCK)
        kpos = own * MOBA_BLOCK + jnp.arange(MOBA_BLOCK)
        s_own = jnp.where(kpos[None, :] <= qpos[:, None], s_own, -jnp.inf)
        p = jax.nn.softmax(jnp.concatenate([s_sel, s_own], axis=-1), axis=-1).astype(v.dtype)
        p_sel = p[..., :top * MOBA_BLOCK].reshape(n, Q_BLOCK, top, MOBA_BLOCK)
        p_own = p[..., top * MOBA_BLOCK:]
        return jnp.einsum('nqkj,nqkjd->nqd', p_sel, vs) + jnp.einsum('nqj,njd->nqd', p_own, vo)

    out = lax.map(one, (q_all, sel_all, val_all, jnp.arange(nq)))
    return out.transpose(1, 0, 2, 3).reshape(b, h, s_len, d)


def token_mixer(hn, w_in, b_gate, q_norm, w_uq, kv_norm, w_ukv, w_branch, w_out, cos_p, sin_p, cos_m, sin_m):
    bsz, s_len, _ = hn.shape
    proj = hn @ w_in
    moba_qkv, sb_qkv, c_q, c_kv, k_rope, gates = jnp.split(
        proj, [SPLIT_1, SPLIT_2, SPLIT_3, SPLIT_4, SPLIT_5], axis=-1)
    qa, ka, va = [to_heads(t, MOBA_HEADS) for t in jnp.split(moba_qkv, 3, axis=-1)]
    y_a = from_heads(moba_attention(partial_rope(qa, cos_p, sin_p), partial_rope(ka, cos_p, sin_p), va))
    qb_, kb_, vb_ = [to_heads(t, SB_HEADS) for t in jnp.split(sb_qkv, 3, axis=-1)]
    y_b = from_heads(stick_breaking_attention(qb_, kb_, vb_))
    qc = to_heads(rms_norm(c_q, q_norm) @ w_uq, MLA_HEADS)
    qc = jnp.concatenate([qc[..., :MLA_NOPE], apply_rope(qc[..., MLA_NOPE:], cos_m, sin_m)], axis=-1)
    kv = to_heads(rms_norm(c_kv, kv_norm) @ w_ukv, MLA_HEADS)
    k_nope, vc = kv[..., :MLA_NOPE], kv[..., MLA_NOPE:]
    kr = jnp.broadcast_to(apply_rope(k_rope, cos_m, sin_m)[:, None], (bsz, MLA_HEADS, s_len, MLA_ROPE))
    y_c = from_heads(causal_softmax_attention(qc, jnp.concatenate([k_nope, kr], axis=-1), vc))
    g = jax.nn.sigmoid(gates.reshape(bsz, s_len, N_BRANCH, D_MODEL) + b_gate)
    ys = jnp.stack([y_a, y_b, y_c], axis=2)
    merged = jnp.einsum('bsnk,nkd,bsnd->bsd', ys, w_branch, g)
    return merged @ w_out


def swiglu(h, w_gate, w_up, w_down):
    return (jax.nn.silu(h @ w_gate) * (h @ w_up)) @ w_down


def modulate(x, g_pre, shift, scale):
    return rms_norm(x, g_pre) * (1 + scale) + shift


def residual_add(x, y, g_post, gate, res_w):
    return x + res_w * (1 + gate) * rms_norm(y, g_post)


def setup_inputs(seed: int = 0) -> dict:
    key = jax.random.key(seed)
    ks = jax.random.split(key, 20)
    f32 = jnp.float32
    nrm = lambda k, shape, fan: jax.random.normal(k, shape, f32) * fan ** -0.5
    return {
        'x': jax.random.normal(ks[0], (BATCH, SEQ, D_MODEL), f32),
        'c': jax.random.normal(ks[1], (BATCH, D_MODEL), f32),
        'ada_w': 0.2 * nrm(ks[2], (DEPTH, D_MODEL, N_SUB * 3 * D_MODEL), D_MODEL),
        'ada_b': 0.01 * jax.random.normal(ks[3], (DEPTH, N_SUB * 3 * D_MODEL), f32),
        'norm_pre': 1.0 + 0.05 * jax.random.normal(ks[4], (DEPTH, N_SUB, D_MODEL), f32),
        'norm_post': 1.0 + 0.05 * jax.random.normal(ks[5], (DEPTH, N_SUB, D_MODEL), f32),
        'ffn_w_gate': nrm(ks[6], (DEPTH, 2, D_MODEL, D_FF), D_MODEL),
        'ffn_w_up': nrm(ks[7], (DEPTH, 2, D_MODEL, D_FF), D_MODEL),
        'ffn_w_down': nrm(ks[8], (DEPTH, 2, D_FF, D_MODEL), D_FF),
        'mix_w_in': nrm(ks[9], (DEPTH, D_MODEL, N_IN), D_MODEL),
        'mix_b_gate': 0.1 * jax.random.normal(ks[10], (DEPTH, N_BRANCH, D_MODEL), f32),
        'mla_q_norm': 1.0 + 0.05 * jax.random.normal(ks[11], (DEPTH, MLA_Q_LORA), f32),
        'mla_w_uq': nrm(ks[12], (DEPTH, MLA_Q_LORA, MLA_HEADS * MLA_QK), MLA_Q_LORA),
        'mla_kv_norm': 1.0 + 0.05 * jax.random.normal(ks[13], (DEPTH, MLA_KV_LORA), f32),
        'mla_w_ukv': nrm(ks[14], (DEPTH, MLA_KV_LORA, MLA_HEADS * (MLA_NOPE + MLA_V)), MLA_KV_LORA),
        'mix_w_branch': nrm(ks[15], (DEPTH, N_BRANCH, BRANCH_W, D_MODEL), BRANCH_W),
        'mix_w_out': nrm(ks[16], (DEPTH, D_MODEL, D_MODEL), D_MODEL),
    }


def reference(x, c, ada_w, ada_b, norm_pre, norm_post, ffn_w_gate, ffn_w_up, ffn_w_down, mix_w_in, mix_b_gate,
              mla_q_norm, mla_w_uq, mla_kv_norm, mla_w_ukv, mix_w_branch, mix_w_out):
    bsz, s_len, _ = x.shape
    cos_p, sin_p = rope_tables(PARTIAL_ROT, s_len)
    cos_m, sin_m = rope_tables(MLA_ROPE, s_len)
    c_act = jax.nn.silu(c)
    for l in range(DEPTH):
        mod = (c_act @ ada_w[l] + ada_b[l]).reshape(bsz, N_SUB, 3, D_MODEL)[:, :, :, None, :]
        h = modulate(x, norm_pre[l, 0], mod[:, 0, 0], mod[:, 0, 1])
        x = residual_add(x, swiglu(h, ffn_w_gate[l, 0], ffn_w_up[l, 0], ffn_w_down[l, 0]), norm_post[l, 0], mod[:, 0, 2], 0.5)
        h = modulate(x, norm_pre[l, 1], mod[:, 1, 0], mod[:, 1, 1])
        y = token_mixer(h, mix_w_in[l], mix_b_gate[l], mla_q_norm[l], mla_w_uq[l], mla_kv_norm[l], mla_w_ukv[l],
                        mix_w_branch[l], mix_w_out[l], cos_p, sin_p, cos_m, sin_m)
        x = residual_add(x, y, norm_post[l, 1], mod[:, 1, 2], 1.0)
        h = modulate(x, norm_pre[l, 2], mod[:, 2, 0], mod[:, 2, 1])
        x = residual_add(x, swiglu(h, ffn_w_gate[l, 1], ffn_w_up[l, 1], ffn_w_down[l, 1]), norm_post[l, 2], mod[:, 2, 2], 0.5)
    return x
```

```python
import numpy as np
import concourse.bass as bass
import concourse.mybir as mybir
from concourse.bass_utils import run_bass_kernel_spmd

F32 = mybir.dt.float32
BF16 = mybir.dt.bfloat16
I32 = mybir.dt.int32
AF = mybir.ActivationFunctionType
ALU = mybir.AluOpType
AX = mybir.AxisListType


class T:
    __slots__ = ("t", "name", "lw", "rd", "dsem", "dcnt")

    def __init__(self, t, name):
        self.t = t
        self.name = name
        self.lw = None
        self.rd = []
        self.dsem = None
        self.dcnt = 0

    def __getitem__(self, idx):
        return self.t[idx]


class Ctx:
    def __init__(self, nc):
        self.nc = nc
        self.stack = []
        self.eng = {"pe": nc.tensor, "act": nc.scalar, "dve": nc.vector, "pool": nc.gpsimd, "sp": nc.sync}
        self.sem = {}
        self.cnt = {}
        for e in self.eng:
            self.sem[e] = self._enter(nc.semaphore("s_" + e))
            self.cnt[e] = 0
        self.seen = {e: {} for e in self.eng}
        self.nt = 0

    def _enter(self, cm):
        v = cm.__enter__()
        self.stack.append(cm)
        return v

    def close(self):
        while self.stack:
            self.stack.pop().__exit__(None, None, None)

    def sb(self, name, shape, dt):
        return T(self._enter(self.nc.sbuf_tensor("sb_" + name, shape, dt)), name)

    def ps(self, name, shape=(128, 512), dt=F32):
        return T(self._enter(self.nc.psum_tensor("ps_" + name, list(shape), dt)), name)

    def dram(self, name, shape, dt, kind):
        t = T(self.nc.dram_tensor(name, list(shape), dt, kind=kind), "dram_" + name)
        return t

    def dsem(self, tile, name=None):
        self.nt += 1
        tile.dsem = self._enter(self.nc.semaphore(name or ("d%d_%s" % (self.nt, tile.name))))
        tile.dcnt = 0
        return tile

    def share_dsem(self, tile, other):
        tile.dsem = other.dsem
        return tile

    def _wait(self, e, deps):
        eng = self.eng[e]
        seen = self.seen[e]
        best = {}
        for (s, v) in deps:
            k = id(s)
            if k not in best or best[k][1] < v:
                best[k] = (s, v)
        for k, (s, v) in best.items():
            if seen.get(k, 0) >= v:
                continue
            eng.wait_ge(s, v)
            seen[k] = v

    def _deps(self, e, reads, writes):
        deps = []
        for t in reads:
            if t.lw is not None:
                deps.append(t.lw)
        me_ = id(self.sem[e])
        for t in writes:
            if t.lw is not None and id(t.lw[0]) != me_:
                deps.append(t.lw)
            deps.extend(d for d in t.rd if id(d[0]) != me_)
        if e == "pe":
            me = id(self.sem["pe"])
            deps = [d for d in deps if id(d[0]) != me]
        return deps

    def op(self, e, fn, reads=(), writes=()):
        self._wait(e, self._deps(e, reads, writes))
        ins = fn()
        self.cnt[e] += 1
        tok = (self.sem[e], self.cnt[e])
        ins.then_inc(self.sem[e], 1)
        for t in reads:
            t.rd.append(tok)
            if len(t.rd) > 6:
                t.rd = self._compact(t.rd)
        for t in writes:
            t.lw = tok
            t.rd = []
        return ins

    def _compact(self, rd):
        best = {}
        for (s, v) in rd:
            k = id(s)
            if k not in best or best[k][1] < v:
                best[k] = (s, v)
        return list(best.values())

    def dma(self, q, out_ap, in_ap, reads=(), writes=(), sem_tile=None, **kw):
        self._wait(q, self._deps(q, reads, writes))
        st = sem_tile
        if st is None:
            for t in list(writes) + list(reads):
                if t.dsem is not None:
                    st = t
                    break
        assert st is not None and st.dsem is not None, "dma needs a tile with dsem"
        ins = self.eng[q].dma_start(out=out_ap, in_=in_ap, **kw)
        st.dcnt += 16
        ins.then_inc(st.dsem, 16)
        tok = (st.dsem, st.dcnt)
        for t in reads:
            t.rd.append(tok)
        for t in writes:
            t.lw = tok
            t.rd = []
        return ins

    def wait_all(self, q, tiles):
        deps = []
        for t in tiles:
            if t.lw is not None:
                deps.append(t.lw)
            deps.extend(t.rd)
        self._wait(q, deps)


D = 1024
SEQ = 16384
NCORE = 8
TPC = SEQ // NCORE
TT = 512
DFF = 2816
FFC = DFF // 128
EPS = 1e-6
THETA = 500000.0
NEG = -30000.0

R_MQ, R_MK, R_MV = 0, 512, 1024
R_SQ, R_SK, R_SV = 1536, 2048, 2560
R_CQN, R_CQR, R_CKN, R_CV, R_KR = 3072, 3584, 3840, 4352, 4864
NROW_A = 4896


def _act(c, out, in_, func, reads, writes, **kw):
    nc = c.nc
    return c.op("act", lambda: nc.scalar.activation(out=out, in_=in_, func=func, **kw), reads=reads, writes=writes)


class Common:
    def __init__(self, c):
        self.c = c
        nc = c.nc
        self.ones = c.sb("ones", [128, 128], BF16)
        c.op("dve", lambda: nc.vector.memset(self.ones[:], 1.0), writes=[self.ones])
        self.wbuf = [c.dsem(c.sb("wbuf%d" % i, [128, FFC, 128], BF16)) for i in range(3)]
        self.wi = 0
        self.psl = [c.ps("psl%d" % i) for i in range(5)]
        self.pi = 0
        self.pstat = c.ps("pstat")
        self.sq = c.sb("sq", [128, 8, TT], BF16)
        self.lnt = c.sb("lnt", [128, TT], F32)

    def next_ps(self):
        p = self.psl[self.pi % len(self.psl)]
        self.pi += 1
        return p

    def lin(self, w_ap, KC, nsz, in_tile, in_ap_fn, cb, coff=0):
        c, nc = self.c, self.c.nc
        wb = self.wbuf[self.wi % 3]
        self.wi += 1
        c.dma("pool", wb[:, 0:KC, :], w_ap.rearrange("p (kc n) -> p kc n", kc=KC), writes=[wb])
        ps = self.next_ps()
        for kc in range(KC):
            c.op("pe", lambda kc=kc: nc.tensor.matmul(ps[0:nsz, :], lhsT=wb[:, kc, coff:coff + nsz], rhs=in_ap_fn(kc),
                                                       start=(kc == 0), stop=(kc == KC - 1)),
                 reads=[wb, in_tile], writes=[ps])
        cb(ps)

    def rstd(self, src_tile, src_ap_fn, KC, F, out_tile, npart=128):
        c, nc = self.c, self.c.nc
        for kc in range(KC):
            _act(c, self.sq[0:npart, kc, :], src_ap_fn(kc), AF.Square, [src_tile], [self.sq])
        for kc in range(KC):
            c.op("pe", lambda kc=kc: nc.tensor.matmul(self.pstat[:, :], lhsT=self.ones[0:npart, :], rhs=self.sq[0:npart, kc, :],
                                                       start=(kc == 0), stop=(kc == KC - 1)),
                 reads=[self.ones, self.sq], writes=[self.pstat])
        _act(c, self.lnt[:], self.pstat[:], AF.Ln, [self.pstat], [self.lnt], scale=1.0 / F, bias=self.epsb[:, 0:1])
        _act(c, out_tile[:, 0:TT], self.lnt[:], AF.Exp, [self.lnt], [out_tile], scale=-0.5)

    def setup_mod(self, cvec_d, adaw_d, adab_d, npre_d, npost_d):
        c, nc = self.c, self.c.nc
        self.epsb = c.sb("epsb", [128, 1], F32)
        c.op("dve", lambda: nc.vector.memset(self.epsb[:], EPS), writes=[self.epsb])
        cv = c.dsem(c.sb("cv", [128, 8], F32))
        ca = c.sb("ca", [128, 8], F32)
        c.dma("sp", cv[:], cvec_d[:, :], writes=[cv])
        _act(c, ca[:], cv[:], AF.Silu, [cv], [ca])
        aw = [c.dsem(c.sb("aw%d" % i, [128, 8, 512], F32)) for i in range(2)]
        pm = self.pstat
        self.modT = c.sb("modT", [128, 72], F32)
        adab = c.dsem(c.sb("adab", [128, 72], F32))
        c.dma("sp", adab[:], adab_d[:, :], writes=[adab])
        for g in range(18):
            a = aw[g % 2]
            c.dma("sp", a[:], adaw_d[:, g * 512:(g + 1) * 512].rearrange("(kc p) n -> p kc n", p=128), writes=[a])
            for j in range(4):
                col = g * 4 + j
                for kc in range(8):
                    c.op("pe", lambda kc=kc, j=j, col=col: nc.tensor.matmul(
                        pm[:, col:col + 1], lhsT=a[:, kc, j * 128:(j + 1) * 128], rhs=ca[:, kc:kc + 1],
                        start=(kc == 0), stop=(kc == 7)), reads=[a, ca], writes=[pm])
        c.op("dve", lambda: nc.vector.tensor_tensor(out=self.modT[:], in0=pm[:, 0:72], in1=adab[:], op=ALU.add),
             reads=[pm, adab], writes=[self.modT])
        npre = c.dsem(c.sb("npre", [128, 24], F32))
        npost = c.dsem(c.sb("npost", [128, 24], F32))
        c.dma("sp", npre[:], npre_d[:, :], writes=[npre])
        c.dma("sp", npost[:], npost_d[:, :], writes=[npost])
        self.va = c.sb("va", [128, 24], F32)
        self.vg = c.sb("vg", [128, 24], F32)
        for s, rw in ((0, 0.5), (1, 1.0), (2, 0.5)):
            sh, sc, gt = (s * 3 + 0) * 8, (s * 3 + 1) * 8, (s * 3 + 2) * 8
            c.op("dve", lambda s=s, sc=sc: nc.vector.scalar_tensor_tensor(
                out=self.va[:, s * 8:s * 8 + 8], in0=self.modT[:, sc:sc + 8], scalar=1.0, in1=npre[:, s * 8:s * 8 + 8],
                op0=ALU.add, op1=ALU.mult), reads=[self.modT, npre], writes=[self.va])
            c.op("dve", lambda s=s, gt=gt: nc.vector.scalar_tensor_tensor(
                out=self.vg[:, s * 8:s * 8 + 8], in0=self.modT[:, gt:gt + 8], scalar=1.0, in1=npost[:, s * 8:s * 8 + 8],
                op0=ALU.add, op1=ALU.mult), reads=[self.modT, npost], writes=[self.vg])
            if rw != 1.0:
                c.op("dve", lambda s=s, rw=rw: nc.vector.tensor_scalar(
                    out=self.vg[:, s * 8:s * 8 + 8], in0=self.vg[:, s * 8:s * 8 + 8], scalar1=rw, scalar2=None, op0=ALU.mult),
                    reads=[self.vg], writes=[self.vg])

    def vb(self, s, kc):
        col = (s * 3) * 8 + kc
        return self.modT[:, col:col + 1]

    def modulate(self, s, x_tile, x_ap_fn, R, tmp, h_tile):
        c, nc = self.c, self.c.nc
        for kc in range(8):
            c.op("dve", lambda kc=kc: nc.vector.tensor_tensor(out=tmp[:, kc, :], in0=x_ap_fn(kc), in1=R[:, 0:TT], op=ALU.mult),
                 reads=[x_tile, R], writes=[tmp])
            _act(c, h_tile[:, kc, :], tmp[:, kc, :], AF.Identity, [tmp, self.va, self.modT], [h_tile],
                 scale=self.va[:, s * 8 + kc:s * 8 + kc + 1], bias=self.vb(s, kc))

    def residual(self, s, x_tile, y_tile, R, tmp):
        c, nc = self.c, self.c.nc
        for kc in range(8):
            c.op("dve", lambda kc=kc: nc.vector.tensor_tensor(out=tmp[:, kc, :], in0=y_tile[:, kc, :], in1=R[:, 0:TT], op=ALU.mult),
                 reads=[y_tile, R], writes=[tmp])
            c.op("dve", lambda kc=kc: nc.vector.scalar_tensor_tensor(
                out=x_tile[:, kc, :], in0=tmp[:, kc, :], scalar=self.vg[:, s * 8 + kc:s * 8 + kc + 1], in1=x_tile[:, kc, :],
                op0=ALU.mult, op1=ALU.add), reads=[tmp, self.vg, x_tile], writes=[x_tile])

    def ffn(self, s, wg_d, wu_d, wd_d, x_tile, h_tile, act_tile, y_tile, tmp, R):
        c, nc = self.c, self.c.nc
        self.rstd(x_tile, lambda kc: x_tile[:, kc, :], 8, D, R)
        self.modulate(s, x_tile, lambda kc: x_tile[:, kc, :], R, tmp, h_tile)
        sg = self.lnt
        for j in range(FFC):
            def cb_g(ps):
                _act(c, sg[:], ps[:, :], AF.Silu, [ps], [sg])
            self.lin(wg_d[j], 8, 128, h_tile, lambda kc: h_tile[:, kc, :], cb_g)
            def cb_u(ps, j=j):
                c.op("dve", lambda: nc.vector.tensor_tensor(out=act_tile[:, j, :], in0=ps[:, :], in1=sg[:], op=ALU.mult),
                     reads=[ps, sg], writes=[act_tile])
            self.lin(wu_d[j], 8, 128, h_tile, lambda kc: h_tile[:, kc, :], cb_u)
        for dc in range(8):
            def cb_d(ps, dc=dc):
                _act(c, y_tile[:, dc, :], ps[:, :], AF.Copy, [ps], [y_tile])
            self.lin(wd_d[dc], FFC, 128, act_tile, lambda kc: act_tile[:, kc, :], cb_d)
        self.rstd(y_tile, lambda kc: y_tile[:, kc, :], 8, D, R)
        self.residual(s, x_tile, y_tile, R, tmp)


WA_MQR, WA_MQS, WA_MQP, WA_MKR, WA_MKS, WA_MKP, WA_MV = 0, 128, 256, 640, 768, 896, 1280
WA_SQ, WA_SK, WA_SV, WA_CQ, WA_CKV, WA_KRR, WA_KRS = 1792, 2304, 2816, 3328, 3584, 3712, 3744
NWA = 3776


def build_A():
    nc = bass.Bass("TRN2", target_bir_lowering=False)
    c = Ctx(nc)
    di = lambda n, s, dt=F32: c.dram(n, s, dt, "ExternalInput")
    xT_d = di("xT", [D, TPC]); cvec_d = di("cvec", [128, 8]); adaw_d = di("adaw", [D, 9216])
    adab_d = di("adab", [128, 72]); npre_d = di("npre", [128, 24]); npost_d = di("npost", [128, 24])
    wg_d = di("wg", [FFC, 128, 1024]); wu_d = di("wu", [FFC, 128, 1024]); wd_d = di("wd", [8, 128, DFF])
    wa_d = di("wa", [30, 128, 1024]); wuq_d = di("wuq", [8, 128, 256]); wukv_d = di("wukv", [8, 128, 128])
    qn_d = di("qn", [128, 2]); kvn_d = di("kvn", [128, 1])
    tab_d = di("tab", [4, 128, TPC])
    x1_d = c.dsem(c.dram("x1T", [D, TPC], F32, "ExternalOutput"))
    hm_d = c.dsem(c.dram("hmT", [D, TPC], BF16, "ExternalOutput"))
    oa_d = c.dram("oa", [NROW_A, TPC], BF16, "ExternalOutput")
    cm = Common(c)
    cm.setup_mod(cvec_d, adaw_d, adab_d, npre_d, npost_d)
    qn = c.dsem(c.sb("qn", [128, 2], F32)); kvn = c.dsem(c.sb("kvn", [128, 1], F32))
    c.dma("sp", qn[:], qn_d[:, :], writes=[qn]); c.dma("sp", kvn[:], kvn_d[:, :], writes=[kvn])
    tab = c.dsem(c.sb("tab", [128, 4, TT], F32))
    x = c.dsem(c.sb("x", [128, 8, TT], F32))
    h = c.dsem(c.sb("h", [128, 8, TT], BF16))
    act = c.sb("act", [128, FFC, TT], BF16)
    y = c.sb("y", [128, 8, TT], F32)
    tmp = c.sb("tmp", [128, 8, TT], F32)
    R = c.sb("R", [128, TT], F32)
    ob = [c.dsem(c.sb("ob%d" % i, [128, TT], BF16)) for i in range(3)]
    obi = [0]
    rr = c.sb("rr", [128, TT], F32)
    cq = c.sb("cq", [128, 2, TT], F32)
    cqn = c.sb("cqn", [128, 2, TT], BF16)

    for t in range(TPC // TT):
        tsl = slice(t * TT, (t + 1) * TT)
        c.dma("sp", x[:], xT_d[:, tsl].rearrange("(kc p) t -> p kc t", p=128), writes=[x])
        c.dma("sp", tab[:], tab_d[:, :, tsl].rearrange("f p t -> p f t"), writes=[tab])
        cm.ffn(0, wg_d, wu_d, wd_d, x, h, act, y, tmp, R)
        c.dma("sp", x1_d[:, tsl].rearrange("(kc p) t -> p kc t", p=128), x[:], reads=[x], writes=[x1_d])
        cm.rstd(x, lambda kc: x[:, kc, :], 8, D, R)
        cm.modulate(1, x, lambda kc: x[:, kc, :], R, tmp, h)
        c.dma("sp", hm_d[:, tsl].rearrange("(kc p) t -> p kc t", p=128), h[:], reads=[h], writes=[hm_d], sem_tile=h)

        def out_rows(row0, nsz, src_fn):
            o = ob[obi[0] % 3]; obi[0] += 1
            src_fn(o)
            c.dma("sp", oa_d[row0:row0 + nsz, tsl], o[0:nsz, :], reads=[o], sem_tile=o)

        def copy_out(row0, nsz=128):
            def cb(ps):
                out_rows(row0, nsz, lambda o: _act(c, o[0:nsz, :], ps[0:nsz, :], AF.Copy, [ps], [o]))
            return cb

        def rope_pair(w_d, KC, in_tile, in_fn, col_r, col_s, nsz, row0, ftab):
            def cb_r(ps):
                c.op("dve", lambda: nc.vector.tensor_tensor(out=rr[0:nsz, :], in0=ps[0:nsz, :], in1=tab[0:nsz, ftab, :], op=ALU.mult),
                     reads=[ps, tab], writes=[rr])
            cm.lin(w_d[col_r // 128], KC, nsz, in_tile, in_fn, cb_r, coff=col_r % 128)
            def cb_s(ps):
                c.op("dve", lambda: nc.vector.tensor_tensor(out=cm.lnt[0:nsz, :], in0=ps[0:nsz, :], in1=tab[0:nsz, ftab + 1, :], op=ALU.mult),
                     reads=[ps, tab], writes=[cm.lnt])
                out_rows(row0, nsz, lambda o: c.op("dve", lambda: nc.vector.tensor_tensor(
                    out=o[0:nsz, :], in0=rr[0:nsz, :], in1=cm.lnt[0:nsz, :], op=ALU.add), reads=[rr, cm.lnt], writes=[o]))
            cm.lin(w_d[col_s // 128], KC, nsz, in_tile, in_fn, cb_s, coff=col_s % 128)

        hf = lambda kc: h[:, kc, :]
        rope_pair(wa_d, 8, h, hf, WA_MQR, WA_MQS, 128, R_MQ, 0)
        for j in range(3):
            cm.lin(wa_d[WA_MQP // 128 + j], 8, 128, h, hf, copy_out(R_MQ + 128 + j * 128))
        rope_pair(wa_d, 8, h, hf, WA_MKR, WA_MKS, 128, R_MK, 0)
        for j in range(3):
            cm.lin(wa_d[WA_MKP // 128 + j], 8, 128, h, hf, copy_out(R_MK + 128 + j * 128))
        for j in range(4):
            cm.lin(wa_d[WA_MV // 128 + j], 8, 128, h, hf, copy_out(R_MV + j * 128))
        for j in range(12):
            cm.lin(wa_d[WA_SQ // 128 + j], 8, 128, h, hf, copy_out(R_SQ + j * 128))
        for j in range(2):
            def cb(ps, j=j):
                _act(c, cq[:, j, :], ps[:, :], AF.Copy, [ps], [cq])
            cm.lin(wa_d[WA_CQ // 128 + j], 8, 128, h, hf, cb)
        cm.rstd(cq, lambda kc: cq[:, kc, :], 2, 256, R)
        for j in range(2):
            c.op("dve", lambda j=j: nc.vector.tensor_tensor(out=cq[:, j, :], in0=cq[:, j, :], in1=R[:, 0:TT], op=ALU.mult),
                 reads=[cq, R], writes=[cq])
            _act(c, cqn[:, j, :], cq[:, j, :], AF.Identity, [cq, qn], [cqn], scale=qn[:, j:j + 1])
        qf = lambda kc: cqn[:, kc, :]
        for j in range(4):
            cm.lin(wuq_d[j], 2, 128, cqn, qf, copy_out(R_CQN + j * 128))
        for j in range(2):
            rope_pair(wuq_d, 2, cqn, qf, 512 + j * 128, 768 + j * 128, 128, R_CQR + j * 128, 2)
        def cbkv(ps):
            _act(c, cq[:, 0, :], ps[:, :], AF.Copy, [ps], [cq])
        cm.lin(wa_d[WA_CKV // 128], 8, 128, h, hf, cbkv)
        cm.rstd(cq, lambda kc: cq[:, kc, :], 1, 128, R)
        c.op("dve", lambda: nc.vector.tensor_tensor(out=cq[:, 0, :], in0=cq[:, 0, :], in1=R[:, 0:TT], op=ALU.mult),
             reads=[cq, R], writes=[cq])
        _act(c, cqn[:, 0, :], cq[:, 0, :], AF.Identity, [cq, kvn], [cqn], scale=kvn[:, 0:1])
        for j in range(8):
            cm.lin(wukv_d[j], 1, 128, cqn, qf, copy_out(R_CKN + j * 128))
        rope_pair(wa_d, 8, h, hf, WA_KRR, WA_KRS, 32, R_KR, 2)
    c.wait_all("sp", [x1_d, hm_d, x, h] + ob)
    c.close()
    return nc


NKT = SEQ // 128
NQT = SEQ // TT


def build_B():
    nc = bass.Bass("TRN2", target_bir_lowering=False)
    c = Ctx(nc)
    di = lambda n, s, dt=BF16: c.dram(n, s, dt, "ExternalInput")
    q_d = [di("q%d" % i, [dk, SEQ]) for i, dk in enumerate((64, 64, 96))]
    k_d = [di("k%d" % i, [dk, SEQ]) for i, dk in enumerate((64, 64, 96))]
    v_d = [di("v%d" % i, [64, SEQ]) for i in range(3)]
    eblk_d = di("eblk", [64, SEQ])
    ident_d = di("ident", [128, 128])
    cm_d = di("cmask", [2, 4, 128, TT])
    tri_d = di("tri", [128, 128])
    y_d = c.dram("yT", [3, 64, SEQ], F32, "ExternalOutput")

    qa = c.dsem(c.sb("qa", [128, SEQ], BF16))
    ka = c.dsem(c.sb("ka", [128, SEQ], BF16))
    vt = c.dsem(c.sb("vt", [64, SEQ], BF16))
    vaug = c.sb("vaug", [128, NKT, 65], BF16)
    ident = c.dsem(c.sb("ident", [128, 128], BF16))
    cmask = c.dsem(c.sb("cmask", [128, 8, TT], BF16))
    tri = c.dsem(c.sb("tri", [128, 128], BF16))
    ones8 = c.sb("ones8", [128, 128], BF16)
    onesf = c.sb("onesf", [128, 64], F32)
    c.dma("sp", ident[:], ident_d[:, :], writes=[ident])
    c.dma("sp", cmask[:], cm_d.t.ap().rearrange("a m p t -> p (a m) t"), writes=[cmask])
    c.dma("sp", tri[:], tri_d[:, :], writes=[tri])
    c.op("dve", lambda: nc.vector.memset(ones8[:], -8.0), writes=[ones8])
    c.op("dve", lambda: nc.vector.memset(onesf[:], 1.0), writes=[onesf])
    c.op("dve", lambda: nc.vector.memset(vaug[:, :, 64:65], 1.0), writes=[vaug])

    pss = [c.ps("pss%d" % i) for i in range(3)]
    pse = [c.ps("pse%d" % i) for i in range(2)]
    po = c.ps("po")
    pb = c.ps("pb")
    ptr = c.ps("ptr", (128, 1024), BF16)
    pt = [c.sb("pt%d" % i, [128, TT], BF16) for i in range(3)]
    ef = [c.sb("ef%d" % i, [128, TT], F32) for i in range(2)]
    spb = [c.sb("spb%d" % i, [128, TT], BF16) for i in range(3)]
    rsum = c.sb("rsum", [128, TT], F32)
    rsb = [c.sb("rsb%d" % i, [128, TT], BF16) for i in range(2)]
    drow = c.sb("drow", [128, TT], F32)
    dbc = c.sb("dbc", [64, TT], F32)
    yo = [c.dsem(c.sb("yo%d" % i, [64, TT], F32)) for i in range(2)]
    cnt = [0]

    def load(kind):
        dk = (64, 64, 96)[kind]
        c.dma("sp", qa[0:dk, :], q_d[kind][:, :], writes=[qa])
        c.dma("sp", ka[0:dk, :], k_d[kind][:, :], writes=[ka])
        c.dma("sp", vt[:, :], v_d[kind][:, :], writes=[vt])
        if kind == 0:
            c.dma("sp", ka[64:128, :], eblk_d[:, :], writes=[ka])
        for kt in range(NKT):
            c.op("pe", lambda kt=kt: nc.tensor.transpose(ptr[:, 0:64], vt[0:64, kt * 128:(kt + 1) * 128], ident[0:64, 0:64]),
                 reads=[vt, ident], writes=[ptr])
            c.op("dve", lambda kt=kt: nc.vector.tensor_copy(out=vaug[:, kt, 0:64], in_=ptr[:, 0:64]), reads=[ptr], writes=[vaug])

    def moba_gate():
        kms = c.sb("kms", [64, 64], F32)
        kmb = c.sb("kmb", [64, 64], F32)
        qf32 = c.sb("qf32", [64, 128], F32)
        gm = c.sb("gm", [128, 64], F32)
        mx = c.sb("mx", [128, 8], F32)
        mbp = c.sb("mbp", [128, 128], BF16)
        c.op("dve", lambda: nc.vector.tensor_reduce(out=kms[:, :], in_=ka[0:64, :].rearrange("p (b j) -> p b j", j=256),
                                                    op=ALU.add, axis=AX.X), reads=[ka], writes=[kms])
        c.op("dve", lambda: nc.vector.tensor_scalar(out=kmb[:, :], in0=kms[:, :], scalar1=1.0 / 256, scalar2=None, op0=ALU.mult),
             reads=[kms], writes=[kmb])
        c.op("dve", lambda: nc.vector.memset(mbp[:, 0:64], 0.0), writes=[mbp])
        for tt in range(NKT):
            qb = tt // 2
            ps = pss[tt % 3]
            c.op("dve", lambda tt=tt: nc.vector.tensor_copy(out=qf32[:, :], in_=qa[0:64, tt * 128:(tt + 1) * 128]), reads=[qa], writes=[qf32])
            c.op("pe", lambda tt=tt, ps=ps: nc.tensor.matmul(ps[:, 0:64], lhsT=qf32[:, :], rhs=kmb[:, :],
                                                             start=True, stop=True), reads=[qf32, kmb], writes=[ps])
            c.op("dve", lambda: nc.vector.memset(gm[:, :], -1e30), writes=[gm])
            if qb > 0:
                c.op("dve", lambda ps=ps, qb=qb: nc.vector.tensor_copy(out=gm[:, 0:qb], in_=ps[:, 0:qb]), reads=[ps], writes=[gm])
            c.op("dve", lambda: nc.vector.max(out=mx[:, :], in_=gm[:, :]), reads=[gm], writes=[mx])
            c.op("dve", lambda: nc.vector.tensor_scalar(out=mx[:, 2:3], in0=mx[:, 2:3], scalar1=-1e29, scalar2=None, op0=ALU.max),
                 reads=[mx], writes=[mx])
            c.op("dve", lambda: nc.vector.tensor_scalar(out=gm[:, :], in0=gm[:, :], scalar1=mx[:, 2:3], scalar2=None, op0=ALU.is_ge),
                 reads=[gm, mx], writes=[gm])
            c.op("dve", lambda qb=qb: nc.vector.memset(gm[:, qb:qb + 1], 1.0), writes=[gm])
            c.op("dve", lambda: nc.vector.tensor_scalar(out=mbp[:, 64:128], in0=gm[:, :], scalar1=1.0, scalar2=-NEG,
                                                        op0=ALU.subtract, op1=ALU.mult), reads=[gm], writes=[mbp])
            pq = pse[tt % 2]
            c.op("pe", lambda pq=pq: nc.tensor.matmul(pq[:, 0:128], lhsT=mbp[:, :], rhs=ident[:, :], start=True, stop=True),
                 reads=[mbp, ident], writes=[pq])
            c.op("dve", lambda pq=pq, tt=tt: nc.vector.tensor_copy(out=qa[64:128, tt * 128:(tt + 1) * 128], in_=pq[64:128, 0:128]),
                 reads=[pq], writes=[qa])

    def attn(kind):
        dk = (128, 64, 96)[kind]
        scale = (64 ** -0.5, 64 ** -0.5, 96 ** -0.5)[kind]
        sb_mode = (kind == 1)
        mrow = 4 if sb_mode else 0
        nv = 64 if sb_mode else 65
        steps = []
        for qt in range(NQT):
            nk = 4 * qt + 4
            order = list(range(nk - 1, -1, -1)) if sb_mode else list(range(nk))
            for i, kt in enumerate(order):
                steps.append((qt, i, kt, nk))
        N = len(steps)

        def zmm(ps, qt, kt, last_stop):
            qs = slice(qt * TT, (qt + 1) * TT); ks = slice(kt * 128, (kt + 1) * 128)
            diag = kt >= 4 * qt
            c.op("pe", lambda: nc.tensor.matmul(ps[:, :], lhsT=ka[0:dk, ks], rhs=qa[0:dk, qs], start=True, stop=(last_stop and not diag)),
                 reads=[ka, qa], writes=[ps])
            if diag:
                m = kt - 4 * qt
                c.op("pe", lambda: nc.tensor.matmul(ps[:, :], lhsT=ident[:, :], rhs=cmask[:, mrow + m, :], start=False, stop=last_stop),
                     reads=[ident, cmask], writes=[ps])

        def s1a(n):
            qt, i, kt, nk = steps[n]
            ps = pss[n % 3]
            zmm(ps, qt, kt, not sb_mode)
            if not sb_mode:
                _act(c, pt[n % 3][:], ps[:, :], AF.Exp, [ps], [pt[n % 3]], scale=scale)
            else:
                e_t = ef[n % 2]; s_t = spb[n % 3]
                _act(c, e_t[:], ps[:, :], AF.Exp, [ps], [e_t], scale=scale)
                _act(c, s_t[:], e_t[:], AF.Ln, [e_t], [s_t], bias=1.0)

        def s1b(n):
            qt, i, kt, nk = steps[n]
            if not sb_mode or i == nk - 1:
                return
            s_t = spb[n % 3]; rb = rsb[(n + 1) % 2]
            if i == 0:
                c.op("pool", lambda: nc.gpsimd.tensor_copy(out=rsum[:], in_=s_t[:]), reads=[s_t], writes=[rsum])
            else:
                c.op("pool", lambda: nc.gpsimd.tensor_tensor(out=rsum[:], in0=rsum[:], in1=s_t[:], op=ALU.add),
                     reads=[s_t, rsum], writes=[rsum])
            c.op("pool", lambda: nc.gpsimd.tensor_copy(out=rb[:], in_=rsum[:]), reads=[rsum], writes=[rb])

        def s2(n):
            qt, i, kt, nk = steps[n]
            pe_ = pss[n % 3]; s_t = spb[n % 3]; rb = rsb[n % 2]
            c.op("pe", lambda: nc.tensor.matmul(pe_[:, :], lhsT=tri[:, :], rhs=s_t[:], start=False, stop=(i == 0)),
                 reads=[tri, s_t], writes=[pe_])
            if i > 0:
                c.op("pe", lambda: nc.tensor.matmul(pe_[:, :], lhsT=ones8[:, :], rhs=rb[:], start=False, stop=True),
                     reads=[ones8, rb], writes=[pe_])
            _act(c, pt[n % 3][:], pe_[:, :], AF.Exp, [pe_], [pt[n % 3]], scale=scale)

        def s3(n):
            qt, i, kt, nk = steps[n]
            p_t = pt[n % 3]
            c.op("pe", lambda: nc.tensor.matmul(po[0:nv, :], lhsT=vaug[:, kt, 0:nv], rhs=p_t[:], start=(i == 0), stop=(i == nk - 1)),
                 reads=[vaug, p_t], writes=[po])
            if i == nk - 1:
                finalize(qt)

        def finalize(qt):
            qs = slice(qt * TT, (qt + 1) * TT)
            o = yo[qt % 2]
            if sb_mode:
                _act(c, o[:, :], po[0:64, :], AF.Copy, [po], [o])
            else:
                _act(c, drow[64:65, :], po[64:65, :], AF.Copy, [po], [drow])
                c.op("dve", lambda: nc.vector.reciprocal(out=drow[64:65, :], in_=drow[64:65, :]), reads=[drow], writes=[drow])
                c.op("pe", lambda: nc.tensor.matmul(pb[0:64, :], lhsT=onesf[64:65, 0:64], rhs=drow[64:65, :], start=True, stop=True),
                     reads=[onesf, drow], writes=[pb])
                _act(c, dbc[:, :], pb[0:64, :], AF.Copy, [pb], [dbc])
                c.op("dve", lambda: nc.vector.tensor_tensor(out=o[:, :], in0=po[0:64, :], in1=dbc[:, :], op=ALU.mult),
                     reads=[po, dbc], writes=[o])
            c.dma("sp", y_d[kind, :, qs], o[:, :], reads=[o], sem_tile=o)

        if sb_mode:
            for n in range(N + 2):
                if n < N:
                    s1a(n)
                if 0 <= n - 1 < N:
                    s2(n - 1)
                if 0 <= n - 2 < N:
                    s3(n - 2)
                if n < N:
                    s1b(n)
        else:
            for n in range(N + 1):
                if n < N:
                    s1a(n)
                if 0 <= n - 1 < N:
                    s3(n - 1)

    for kind in range(3):
        load(kind)
        if kind == 0:
            moba_gate()
        attn(kind)
    c.wait_all("sp", yo)
    c.close()
    return nc


def build_C():
    nc = bass.Bass("TRN2", target_bir_lowering=False)
    c = Ctx(nc)
    di = lambda n, s, dt=F32: c.dram(n, s, dt, "ExternalInput")
    xT_d = di("xT", [D, TPC]); hm_d = di("hmT", [D, TPC], BF16); yin_d = di("yin", [1536, TPC])
    cvec_d = di("cvec", [128, 8]); adaw_d = di("adaw", [D, 9216])
    adab_d = di("adab", [128, 72]); npre_d = di("npre", [128, 24]); npost_d = di("npost", [128, 24])
    wg_d = di("wg", [FFC, 128, 1024]); wu_d = di("wu", [FFC, 128, 1024]); wd_d = di("wd", [8, 128, DFF])
    wgate_d = di("wgate", [24, 128, 1024]); bg_d = di("bg", [128, 24])
    wbr_d = di("wbr", [24, 128, 512]); wout_d = di("wout", [8, 128, 1024])
    x3_d = c.dsem(c.dram("x3T", [D, TPC], F32, "ExternalOutput"))
    cm = Common(c)
    cm.setup_mod(cvec_d, adaw_d, adab_d, npre_d, npost_d)
    bg = c.dsem(c.sb("bg", [128, 24], F32))
    c.dma("sp", bg[:], bg_d[:, :], writes=[bg])
    x = c.dsem(c.sb("x", [128, 8, TT], F32))
    h = c.dsem(c.sb("h", [128, 8, TT], BF16))
    yin = c.dsem(c.sb("yin", [128, 12, TT], BF16))
    act = c.sb("act", [128, FFC, TT], BF16)
    y = c.sb("y", [128, 8, TT], F32)
    tmp = c.sb("tmp", [128, 8, TT], F32)
    R = c.sb("R", [128, TT], F32)
    mg = c.sb("mg", [128, 8, TT], BF16)
    bs = c.sb("bs", [128, TT], F32)
    sg = c.sb("sg", [128, TT], F32)
    macc = c.sb("macc", [128, TT], F32)
    for t in range(TPC // TT):
        tsl = slice(t * TT, (t + 1) * TT)
        c.dma("sp", x[:], xT_d[:, tsl].rearrange("(kc p) t -> p kc t", p=128), writes=[x])
        c.dma("sp", h[:], hm_d[:, tsl].rearrange("(kc p) t -> p kc t", p=128), writes=[h])
        c.dma("pool", yin[:], yin_d[:, tsl].rearrange("(kc p) t -> p kc t", p=128), writes=[yin])
        for dc in range(8):
            for n in range(3):
                def cb_b(ps):
                    _act(c, bs[:], ps[:, :], AF.Copy, [ps], [bs])
                cm.lin(wbr_d[n * 8 + dc], 4, 128, yin, lambda kc, n=n: yin[:, n * 4 + kc, :], cb_b)
                def cb_g(ps, n=n, dc=dc):
                    _act(c, sg[:], ps[:, :], AF.Sigmoid, [ps, bg], [sg], bias=bg[:, n * 8 + dc:n * 8 + dc + 1])
                cm.lin(wgate_d[n * 8 + dc], 8, 128, h, lambda kc: h[:, kc, :], cb_g)
                if n == 0:
                    c.op("dve", lambda: nc.vector.tensor_tensor(out=macc[:], in0=bs[:], in1=sg[:], op=ALU.mult),
                         reads=[bs, sg], writes=[macc])
                else:
                    c.op("dve", lambda: nc.vector.tensor_tensor(out=bs[:], in0=bs[:], in1=sg[:], op=ALU.mult),
                         reads=[bs, sg], writes=[bs])
                    c.op("dve", lambda: nc.vector.tensor_tensor(out=macc[:], in0=macc[:], in1=bs[:], op=ALU.add),
                         reads=[bs, macc], writes=[macc])
            c.op("dve", lambda dc=dc: nc.vector.tensor_copy(out=mg[:, dc, :], in_=macc[:]), reads=[macc], writes=[mg])
        for dc in range(8):
            def cb_o(ps, dc=dc):
                _act(c, y[:, dc, :], ps[:, :], AF.Copy, [ps], [y])
            cm.lin(wout_d[dc], 8, 128, mg, lambda kc: mg[:, kc, :], cb_o)
        cm.rstd(y, lambda kc: y[:, kc, :], 8, D, R)
        cm.residual(1, x, y, R, tmp)
        cm.ffn(2, wg_d, wu_d, wd_d, x, h, act, y, tmp, R)
        c.dma("sp", x3_d[:, tsl].rearrange("(kc p) t -> p kc t", p=128), x[:], reads=[x], writes=[x3_d])
    c.wait_all("sp", [x3_d, x])
    c.close()
    return nc


def _vec8(v):
    return np.ascontiguousarray(np.asarray(v, np.float32).reshape(-1, 8, 128).transpose(2, 0, 1).reshape(128, -1))


def _rope_consts():
    import ml_dtypes
    pos = np.arange(SEQ, dtype=np.float32)
    def tabs(n_rot, reps):
        inv = np.power(np.float32(THETA), -np.arange(0, n_rot, 2, dtype=np.float32) / np.float32(n_rot)).astype(np.float32)
        ang = (pos[:, None] * inv[None, :]).astype(np.float32)
        co, si = np.cos(ang).astype(np.float32), np.sin(ang).astype(np.float32)
        cos_rows = np.concatenate([co, co], axis=1).T
        sin_rows = np.concatenate([-si, si], axis=1).T
        return np.tile(cos_rows, (reps, 1)), np.tile(sin_rows, (reps, 1))
    cp, sp_ = tabs(16, 8)
    cmm, sm = tabs(32, 4)
    return np.ascontiguousarray(np.stack([cp, sp_, cmm, sm]).astype(np.float32))


def _consts_B():
    import ml_dtypes
    bf = ml_dtypes.bfloat16
    eblk = (np.arange(SEQ)[None, :] // 256 == np.arange(64)[:, None]).astype(np.float32).astype(bf)
    ident = np.eye(128, dtype=np.float32).astype(bf)
    k = np.arange(128)[:, None]; t = np.arange(TT)[None, :]
    cmk = np.zeros((2, 4, 128, TT), np.float32)
    for m in range(4):
        cmk[0, m] = np.where(128 * m + k <= t, 0.0, NEG)
        cmk[1, m] = np.where(128 * m + k < t, 0.0, NEG)
    j = np.arange(128)[:, None]; s = np.arange(128)[None, :]
    tri = np.where(j >= s, -8.0, 0.0).astype(np.float32).astype(bf)
    return dict(eblk=eblk, ident=ident, cmask=cmk.astype(bf), tri=tri)


def _perm_wa(w_in):
    h = np.arange(8)[:, None]
    def cols(base, js):
        return (base + h * 64 + np.asarray(js)[None, :]).reshape(-1)
    r = np.arange(16); s = (r + 8) % 16; p = np.arange(16, 64)
    idx = []
    for base in (0, 512):
        idx += [cols(base, r), cols(base, s), cols(base, p)]
    idx += [np.arange(1024, 1536), np.arange(1536, 3072), np.arange(3072, 3328), np.arange(3328, 3456)]
    kr = 3456 + np.arange(32)
    idx += [kr, 3456 + (np.arange(32) + 16) % 32]
    idx = np.concatenate(idx)
    assert idx.shape[0] == NWA
    return np.ascontiguousarray(w_in[:, idx])


def _perm_wuq(w):
    h = np.arange(8)[:, None]
    n = (h * 96 + np.arange(64)[None, :]).reshape(-1)
    r = (h * 96 + 64 + np.arange(32)[None, :]).reshape(-1)
    s = (h * 96 + 64 + ((np.arange(32) + 16) % 32)[None, :]).reshape(-1)
    return np.ascontiguousarray(w[:, np.concatenate([n, r, s])])


def _perm_wukv(w):
    h = np.arange(8)[:, None]
    kn = (h * 128 + np.arange(64)[None, :]).reshape(-1)
    v = (h * 128 + 64 + np.arange(64)[None, :]).reshape(-1)
    return np.ascontiguousarray(w[:, np.concatenate([kn, v])])


def _pre(w):
    w = np.asarray(w, np.float32)
    K, N = w.shape
    if N % 128:
        w = np.concatenate([w, np.zeros((K, 128 - N % 128), np.float32)], axis=1)
        N = w.shape[1]
    KC, NC_ = K // 128, N // 128
    return np.ascontiguousarray(w.reshape(KC, 128, NC_, 128).transpose(2, 1, 0, 3).reshape(NC_, 128, KC * 128))


_NC = {}


def _get(name, fn):
    if name not in _NC:
        _NC[name] = fn()
    return _NC[name]


def kernel(x, c, ada_w, ada_b, norm_pre, norm_post, ffn_w_gate, ffn_w_up, ffn_w_down, mix_w_in, mix_b_gate,
           mla_q_norm, mla_w_uq, mla_kv_norm, mla_w_ukv, mix_w_branch, mix_w_out):
    f = lambda a: np.ascontiguousarray(np.asarray(a, np.float32))
    x = f(x); cores = list(range(NCORE))
    tabs = _rope_consts()
    cb = _consts_B()
    cvec = _vec8(f(c)[0])
    xT = [np.ascontiguousarray(x[0, i * TPC:(i + 1) * TPC, :].T) for i in cores]
    for l in range(2):
        common = dict(cvec=cvec, adaw=f(ada_w[l]), adab=_vec8(f(ada_b[l])), npre=_vec8(f(norm_pre[l])), npost=_vec8(f(norm_post[l])))
        wa = _pre(_perm_wa(f(mix_w_in[l]))); wuq = _pre(_perm_wuq(f(mla_w_uq[l]))); wukv = _pre(_perm_wukv(f(mla_w_ukv[l])))
        qn = np.ascontiguousarray(f(mla_q_norm[l]).reshape(2, 128).T); kvn = np.ascontiguousarray(f(mla_kv_norm[l]).reshape(1, 128).T)
        ins = [dict(common, xT=xT[i], wg=_pre(ffn_w_gate[l, 0]), wu=_pre(ffn_w_up[l, 0]), wd=_pre(ffn_w_down[l, 0]), wa=wa, wuq=wuq, wukv=wukv,
                    qn=qn, kvn=kvn, tab=np.ascontiguousarray(tabs[:, :, i * TPC:(i + 1) * TPC])) for i in cores]
        ra = run_bass_kernel_spmd(_get("A", build_A), ins, core_ids=cores).results
        oa = np.concatenate([np.asarray(r["oa"]) for r in ra], axis=1)
        insb = []
        for hd in cores:
            mq = np.concatenate([oa[R_MQ + hd * 16:R_MQ + hd * 16 + 16], oa[R_MQ + 128 + hd * 48:R_MQ + 128 + hd * 48 + 48]], 0)
            mk = np.concatenate([oa[R_MK + hd * 16:R_MK + hd * 16 + 16], oa[R_MK + 128 + hd * 48:R_MK + 128 + hd * 48 + 48]], 0)
            mv = oa[R_MV + hd * 64:R_MV + hd * 64 + 64]
            sq = oa[R_SQ + hd * 64:R_SQ + hd * 64 + 64]; sk = oa[R_SK + hd * 64:R_SK + hd * 64 + 64]; sv = oa[R_SV + hd * 64:R_SV + hd * 64 + 64]
            cq_ = np.concatenate([oa[R_CQN + hd * 64:R_CQN + hd * 64 + 64], oa[R_CQR + hd * 32:R_CQR + hd * 32 + 32]], 0)
            ck_ = np.concatenate([oa[R_CKN + hd * 64:R_CKN + hd * 64 + 64], oa[R_KR:R_KR + 32]], 0)
            cv_ = oa[R_CV + hd * 64:R_CV + hd * 64 + 64]
            d = dict(cb)
            for i, (q_, k_, v_) in enumerate(((mq, mk, mv), (sq, sk, sv), (cq_, ck_, cv_))):
                d["q%d" % i] = np.ascontiguousarray(q_); d["k%d" % i] = np.ascontiguousarray(k_); d["v%d" % i] = np.ascontiguousarray(v_)
            insb.append(d)
        rb = run_bass_kernel_spmd(_get("B", build_B), insb, core_ids=cores).results
        yall = np.stack([np.asarray(r["yT"]) for r in rb], axis=1)
        yall = yall.reshape(3 * 512, SEQ)
        insc = [dict(common, xT=np.asarray(ra[i]["x1T"]), hmT=np.asarray(ra[i]["hmT"]), yin=np.ascontiguousarray(yall[:, i * TPC:(i + 1) * TPC]),
                     wg=_pre(ffn_w_gate[l, 1]), wu=_pre(ffn_w_up[l, 1]), wd=_pre(ffn_w_down[l, 1]),
                     wgate=_pre(f(mix_w_in[l])[:, 3488:]), bg=_vec8(f(mix_b_gate[l])),
                     wbr=np.concatenate([_pre(f(mix_w_branch[l])[n]) for n in range(3)], axis=0), wout=_pre(mix_w_out[l])) for i in cores]
        rc = run_bass_kernel_spmd(_get("C", build_C), insc, core_ids=cores).results
        xT = [np.asarray(r["x3T"]) for r in rc]
    out = np.concatenate([t.T for t in xT], axis=0)[None]
    return np.ascontiguousarray(out.astype(np.float32))
```

```python
import numpy as np
import concourse.bass as bass
import concourse.mybir as mybir
from concourse.bass_utils import run_bass_kernel_spmd

F32 = mybir.dt.float32
BF16 = mybir.dt.bfloat16
I32 = mybir.dt.int32
AF = mybir.ActivationFunctionType
ALU = mybir.AluOpType
AX = mybir.AxisListType


class T:
    __slots__ = ("t", "name", "lw", "rd", "dsem", "dcnt")

    def __init__(self, t, name):
        self.t = t
        self.name = name
        self.lw = None
        self.rd = []
        self.dsem = None
        self.dcnt = 0

    def __getitem__(self, idx):
        return self.t[idx]


class Ctx:
    def __init__(self, nc):
        self.nc = nc
        self.stack = []
        self.eng = {"pe": nc.tensor, "act": nc.scalar, "dve": nc.vector, "pool": nc.gpsimd, "sp": nc.sync}
        self.sem = {}
        self.cnt = {}
        for e in self.eng:
            self.sem[e] = self._enter(nc.semaphore("s_" + e))
            self.cnt[e] = 0
        self.seen = {e: {} for e in self.eng}
        self.nt = 0

    def _enter(self, cm):
        v = cm.__enter__()
        self.stack.append(cm)
        return v

    def close(self):
        while self.stack:
            self.stack.pop().__exit__(None, None, None)

    def sb(self, name, shape, dt):
        return T(self._enter(self.nc.sbuf_tensor("sb_" + name, shape, dt)), name)

    def ps(self, name, shape=(128, 512), dt=F32):
        return T(self._enter(self.nc.psum_tensor("ps_" + name, list(shape), dt)), name)

    def dram(self, name, shape, dt, kind):
        t = T(self.nc.dram_tensor(name, list(shape), dt, kind=kind), "dram_" + name)
        return t

    def dsem(self, tile, name=None):
        self.nt += 1
        tile.dsem = self._enter(self.nc.semaphore(name or ("d%d_%s" % (self.nt, tile.name))))
        tile.dcnt = 0
        return tile

    def share_dsem(self, tile, other):
        tile.dsem = other.dsem
        return tile

    def _wait(self, e, deps):
        eng = self.eng[e]
        seen = self.seen[e]
        best = {}
        for (s, v) in deps:
            k = id(s)
            if k not in best or best[k][1] < v:
                best[k] = (s, v)
        for k, (s, v) in best.items():
            if seen.get(k, 0) >= v:
                continue
            eng.wait_ge(s, v)
            seen[k] = v

    def _deps(self, e, reads, writes):
        deps = []
        for t in reads:
            if t.lw is not None:
                deps.append(t.lw)
        me_ = id(self.sem[e])
        for t in writes:
            if t.lw is not None and id(t.lw[0]) != me_:
                deps.append(t.lw)
            deps.extend(d for d in t.rd if id(d[0]) != me_)
        if e == "pe":
            me = id(self.sem["pe"])
            deps = [d for d in deps if id(d[0]) != me]
        return deps

    def op(self, e, fn, reads=(), writes=()):
        self._wait(e, self._deps(e, reads, writes))
        ins = fn()
        self.cnt[e] += 1
        tok = (self.sem[e], self.cnt[e])
        ins.then_inc(self.sem[e], 1)
        for t in reads:
            t.rd.append(tok)
            if len(t.rd) > 6:
                t.rd = self._compact(t.rd)
        for t in writes:
            t.lw = tok
            t.rd = []
        return ins

    def _compact(self, rd):
        best = {}
        for (s, v) in rd:
            k = id(s)
            if k not in best or best[k][1] < v:
                best[k] = (s, v)
        return list(best.values())

    def dma(self, q, out_ap, in_ap, reads=(), writes=(), sem_tile=None, **kw):
        self._wait(q, self._deps(q, reads, writes))
        st = sem_tile
        if st is None:
            for t in list(writes) + list(reads):
                if t.dsem is not None:
                    st = t
                    break
        assert st is not None and st.dsem is not None, "dma needs a tile with dsem"
        ins = self.eng[q].dma_start(out=out_ap, in_=in_ap, **kw)
        st.dcnt += 16
        ins.then_inc(st.dsem, 16)
        tok = (st.dsem, st.dcnt)
        for t in reads:
            t.rd.append(tok)
        for t in writes:
            t.lw = tok
            t.rd = []
        return ins

    def wait_all(self, q, tiles):
        deps = []
        for t in tiles:
            if t.lw is not None:
                deps.append(t.lw)
            deps.extend(t.rd)
        self._wait(q, deps)


D = 1024
SEQ = 16384
NCORE = 8
TPC = SEQ // NCORE
TT = 512
DFF = 2816
FFC = DFF // 128
EPS = 1e-6
THETA = 500000.0
NEG = -30000.0

R_MQ, R_MK, R_MV = 0, 512, 1024
R_SQ, R_SK, R_SV = 1536, 2048, 2560
R_CQN, R_CQR, R_CKN, R_CV, R_KR = 3072, 3584, 3840, 4352, 4864
NROW_A = 4896


def _act(c, out, in_, func, reads, writes, **kw):
    nc = c.nc
    return c.op("act", lambda: nc.scalar.activation(out=out, in_=in_, func=func, **kw), reads=reads, writes=writes)


class Common:
    def __init__(self, c):
        self.c = c
        nc = c.nc
        self.ones = c.sb("ones", [128, 128], BF16)
        c.op("dve", lambda: nc.vector.memset(self.ones[:], 1.0), writes=[self.ones])
        self.wbuf = [c.dsem(c.sb("wbuf%d" % i, [128, FFC, 128], BF16)) for i in range(3)]
        self.wi = 0
        self.psl = [c.ps("psl%d" % i) for i in range(5)]
        self.pi = 0
        self.pstat = c.ps("pstat")
        self.sq = c.sb("sq", [128, 8, TT], BF16)
        self.lnt = c.sb("lnt", [128, TT], F32)

    def next_ps(self):
        p = self.psl[self.pi % len(self.psl)]
        self.pi += 1
        return p

    def lin(self, w_ap, KC, nsz, in_tile, in_ap_fn, cb, coff=0):
        c, nc = self.c, self.c.nc
        wb = self.wbuf[self.wi % 3]
        self.wi += 1
        c.dma("pool", wb[:, 0:KC, :], w_ap.rearrange("p (kc n) -> p kc n", kc=KC), writes=[wb])
        ps = self.next_ps()
        for kc in range(KC):
            c.op("pe", lambda kc=kc: nc.tensor.matmul(ps[0:nsz, :], lhsT=wb[:, kc, coff:coff + nsz], rhs=in_ap_fn(kc),
                                                       start=(kc == 0), stop=(kc == KC - 1)),
                 reads=[wb, in_tile], writes=[ps])
        cb(ps)

    def rstd(self, src_tile, src_ap_fn, KC, F, out_tile, npart=128):
        c, nc = self.c, self.c.nc
        for kc in range(KC):
            _act(c, self.sq[0:npart, kc, :], src_ap_fn(kc), AF.Square, [src_tile], [self.sq])
        for kc in range(KC):
            c.op("pe", lambda kc=kc: nc.tensor.matmul(self.pstat[:, :], lhsT=self.ones[0:npart, :], rhs=self.sq[0:npart, kc, :],
                                                       start=(kc == 0), stop=(kc == KC - 1)),
                 reads=[self.ones, self.sq], writes=[self.pstat])
        _act(c, self.lnt[:], self.pstat[:], AF.Ln, [self.pstat], [self.lnt], scale=1.0 / F, bias=self.epsb[:, 0:1])
        _act(c, out_tile[:, 0:TT], self.lnt[:], AF.Exp, [self.lnt], [out_tile], scale=-0.5)

    def setup_mod(self, cvec_d, adaw_d, adab_d, npre_d, npost_d):
        c, nc = self.c, self.c.nc
        self.epsb = c.sb("epsb", [128, 1], F32)
        c.op("dve", lambda: nc.vector.memset(self.epsb[:], EPS), writes=[self.epsb])
        cv = c.dsem(c.sb("cv", [128, 8], F32))
        ca = c.sb("ca", [128, 8], F32)
        c.dma("sp", cv[:], cvec_d[:, :], writes=[cv])
        _act(c, ca[:], cv[:], AF.Silu, [cv], [ca])
        aw = [c.dsem(c.sb("aw%d" % i, [128, 8, 512], F32)) for i in range(2)]
        pm = self.pstat
        self.modT = c.sb("modT", [128, 72], F32)
        adab = c.dsem(c.sb("adab", [128, 72], F32))
        c.dma("sp", adab[:], adab_d[:, :], writes=[adab])
        for g in range(18):
            a = aw[g % 2]
            c.dma("sp", a[:], adaw_d[:, g * 512:(g + 1) * 512].rearrange("(kc p) n -> p kc n", p=128), writes=[a])
            for j in range(4):
                col = g * 4 + j
                for kc in range(8):
                    c.op("pe", lambda kc=kc, j=j, col=col: nc.tensor.matmul(
                        pm[:, col:col + 1], lhsT=a[:, kc, j * 128:(j + 1) * 128], rhs=ca[:, kc:kc + 1],
                        start=(kc == 0), stop=(kc == 7)), reads=[a, ca], writes=[pm])
        c.op("dve", lambda: nc.vector.tensor_tensor(out=self.modT[:], in0=pm[:, 0:72], in1=adab[:], op=ALU.add),
             reads=[pm, adab], writes=[self.modT])
        npre = c.dsem(c.sb("npre", [128, 24], F32))
        npost = c.dsem(c.sb("npost", [128, 24], F32))
        c.dma("sp", npre[:], npre_d[:, :], writes=[npre])
        c.dma("sp", npost[:], npost_d[:, :], writes=[npost])
        self.va = c.sb("va", [128, 24], F32)
        self.vg = c.sb("vg", [128, 24], F32)
        for s, rw in ((0, 0.5), (1, 1.0), (2, 0.5)):
            sh, sc, gt = (s * 3 + 0) * 8, (s * 3 + 1) * 8, (s * 3 + 2) * 8
            c.op("dve", lambda s=s, sc=sc: nc.vector.scalar_tensor_tensor(
                out=self.va[:, s * 8:s * 8 + 8], in0=self.modT[:, sc:sc + 8], scalar=1.0, in1=npre[:, s * 8:s * 8 + 8],
                op0=ALU.add, op1=ALU.mult), reads=[self.modT, npre], writes=[self.va])
            c.op("dve", lambda s=s, gt=gt: nc.vector.scalar_tensor_tensor(
                out=self.vg[:, s * 8:s * 8 + 8], in0=self.modT[:, gt:gt + 8], scalar=1.0, in1=npost[:, s * 8:s * 8 + 8],
                op0=ALU.add, op1=ALU.mult), reads=[self.modT, npost], writes=[self.vg])
            if rw != 1.0:
                c.op("dve", lambda s=s, rw=rw: nc.vector.tensor_scalar(
                    out=self.vg[:, s * 8:s * 8 + 8], in0=self.vg[:, s * 8:s * 8 + 8], scalar1=rw, scalar2=None, op0=ALU.mult),
                    reads=[self.vg], writes=[self.vg])

    def vb(self, s, kc):
        col = (s * 3) * 8 + kc
        return self.modT[:, col:col + 1]

    def modulate(self, s, x_tile, x_ap_fn, R, tmp, h_tile):
        c, nc = self.c, self.c.nc
        for kc in range(8):
            c.op("dve", lambda kc=kc: nc.vector.tensor_tensor(out=tmp[:, kc, :], in0=x_ap_fn(kc), in1=R[:, 0:TT], op=ALU.mult),
                 reads=[x_tile, R], writes=[tmp])
            _act(c, h_tile[:, kc, :], tmp[:, kc, :], AF.Identity, [tmp, self.va, self.modT], [h_tile],
                 scale=self.va[:, s * 8 + kc:s * 8 + kc + 1], bias=self.vb(s, kc))

    def residual(self, s, x_tile, y_tile, R, tmp):
        c, nc = self.c, self.c.nc
        for kc in range(8):
            c.op("dve", lambda kc=kc: nc.vector.tensor_tensor(out=tmp[:, kc, :], in0=y_tile[:, kc, :], in1=R[:, 0:TT], op=ALU.mult),
                 reads=[y_tile, R], writes=[tmp])
            c.op("dve", lambda kc=kc: nc.vector.scalar_tensor_tensor(
                out=x_tile[:, kc, :], in0=tmp[:, kc, :], scalar=self.vg[:, s * 8 + kc:s * 8 + kc + 1], in1=x_tile[:, kc, :],
                op0=ALU.mult, op1=ALU.add), reads=[tmp, self.vg, x_tile], writes=[x_tile])

    def ffn(self, s, wg_d, wu_d, wd_d, x_tile, h_tile, act_tile, y_tile, tmp, R):
        c, nc = self.c, self.c.nc
        self.rstd(x_tile, lambda kc: x_tile[:, kc, :], 8, D, R)
        self.modulate(s, x_tile, lambda kc: x_tile[:, kc, :], R, tmp, h_tile)
        sg = self.lnt
        for j in range(FFC):
            def cb_g(ps):
                _act(c, sg[:], ps[:, :], AF.Silu, [ps], [sg])
            self.lin(wg_d[j], 8, 128, h_tile, lambda kc: h_tile[:, kc, :], cb_g)
            def cb_u(ps, j=j):
                c.op("dve", lambda: nc.vector.tensor_tensor(out=act_tile[:, j, :], in0=ps[:, :], in1=sg[:], op=ALU.mult),
                     reads=[ps, sg], writes=[act_tile])
            self.lin(wu_d[j], 8, 128, h_tile, lambda kc: h_tile[:, kc, :], cb_u)
        for dc in range(8):
            def cb_d(ps, dc=dc):
                _act(c, y_tile[:, dc, :], ps[:, :], AF.Copy, [ps], [y_tile])
            self.lin(wd_d[dc], FFC, 128, act_tile, lambda kc: act_tile[:, kc, :], cb_d)
        self.rstd(y_tile, lambda kc: y_tile[:, kc, :], 8, D, R)
        self.residual(s, x_tile, y_tile, R, tmp)


WA_MQR, WA_MQS, WA_MQP, WA_MKR, WA_MKS, WA_MKP, WA_MV = 0, 128, 256, 640, 768, 896, 1280
WA_SQ, WA_SK, WA_SV, WA_CQ, WA_CKV, WA_KRR, WA_KRS = 1792, 2304, 2816, 3328, 3584, 3712, 3744
NWA = 3776


def build_A():
    nc = bass.Bass("TRN2", target_bir_lowering=False)
    c = Ctx(nc)
    di = lambda n, s, dt=F32: c.dram(n, s, dt, "ExternalInput")
    xT_d = di("xT", [D, TPC]); cvec_d = di("cvec", [128, 8]); adaw_d = di("adaw", [D, 9216])
    adab_d = di("adab", [128, 72]); npre_d = di("npre", [128, 24]); npost_d = di("npost", [128, 24])
    wg_d = di("wg", [FFC, 128, 1024]); wu_d = di("wu", [FFC, 128, 1024]); wd_d = di("wd", [8, 128, DFF])
    wa_d = di("wa", [30, 128, 1024]); wuq_d = di("wuq", [8, 128, 256]); wukv_d = di("wukv", [8, 128, 128])
    qn_d = di("qn", [128, 2]); kvn_d = di("kvn", [128, 1])
    tab_d = di("tab", [4, 128, TPC])
    x1_d = c.dsem(c.dram("x1T", [D, TPC], F32, "ExternalOutput"))
    hm_d = c.dsem(c.dram("hmT", [D, TPC], BF16, "ExternalOutput"))
    oa_d = c.dram("oa", [NROW_A, TPC], BF16, "ExternalOutput")
    cm = Common(c)
    cm.setup_mod(cvec_d, adaw_d, adab_d, npre_d, npost_d)
    qn = c.dsem(c.sb("qn", [128, 2], F32)); kvn = c.dsem(c.sb("kvn", [128, 1], F32))
    c.dma("sp", qn[:], qn_d[:, :], writes=[qn]); c.dma("sp", kvn[:], kvn_d[:, :], writes=[kvn])
    tab = c.dsem(c.sb("tab", [128, 4, TT], F32))
    x = c.dsem(c.sb("x", [128, 8, TT], F32))
    h = c.dsem(c.sb("h", [128, 8, TT], BF16))
    act = c.sb("act", [128, FFC, TT], BF16)
    y = c.sb("y", [128, 8, TT], F32)
    tmp = c.sb("tmp", [128, 8, TT], F32)
    R = c.sb("R", [128, TT], F32)
    ob = [c.dsem(c.sb("ob%d" % i, [128, TT], BF16)) for i in range(3)]
    obi = [0]
    rr = c.sb("rr", [128, TT], F32)
    cq = c.sb("cq", [128, 2, TT], F32)
    cqn = c.sb("cqn", [128, 2, TT], BF16)

    for t in range(TPC // TT):
        tsl = slice(t * TT, (t + 1) * TT)
        c.dma("sp", x[:], xT_d[:, tsl].rearrange("(kc p) t -> p kc t", p=128), writes=[x])
        c.dma("sp", tab[:], tab_d[:, :, tsl].rearrange("f p t -> p f t"), writes=[tab])
        cm.ffn(0, wg_d, wu_d, wd_d, x, h, act, y, tmp, R)
        c.dma("sp", x1_d[:, tsl].rearrange("(kc p) t -> p kc t", p=128), x[:], reads=[x], writes=[x1_d])
        cm.rstd(x, lambda kc: x[:, kc, :], 8, D, R)
        cm.modulate(1, x, lambda kc: x[:, kc, :], R, tmp, h)
        c.dma("sp", hm_d[:, tsl].rearrange("(kc p) t -> p kc t", p=128), h[:], reads=[h], writes=[hm_d], sem_tile=h)

        def out_rows(row0, nsz, src_fn):
            o = ob[obi[0] % 3]; obi[0] += 1
            src_fn(o)
            c.dma("sp", oa_d[row0:row0 + nsz, tsl], o[0:nsz, :], reads=[o], sem_tile=o)

        def copy_out(row0, nsz=128):
            def cb(ps):
                out_rows(row0, nsz, lambda o: _act(c, o[0:nsz, :], ps[0:nsz, :], AF.Copy, [ps], [o]))
            return cb

        def rope_pair(w_d, KC, in_tile, in_fn, col_r, col_s, nsz, row0, ftab):
            def cb_r(ps):
                c.op("dve", lambda: nc.vector.tensor_tensor(out=rr[0:nsz, :], in0=ps[0:nsz, :], in1=tab[0:nsz, ftab, :], op=ALU.mult),
                     reads=[ps, tab], writes=[rr])
            cm.lin(w_d[col_r // 128], KC, nsz, in_tile, in_fn, cb_r, coff=col_r % 128)
            def cb_s(ps):
                c.op("dve", lambda: nc.vector.tensor_tensor(out=cm.lnt[0:nsz, :], in0=ps[0:nsz, :], in1=tab[0:nsz, ftab + 1, :], op=ALU.mult),
                     reads=[ps, tab], writes=[cm.lnt])
                out_rows(row0, nsz, lambda o: c.op("dve", lambda: nc.vector.tensor_tensor(
                    out=o[0:nsz, :], in0=rr[0:nsz, :], in1=cm.lnt[0:nsz, :], op=ALU.add), reads=[rr, cm.lnt], writes=[o]))
            cm.lin(w_d[col_s // 128], KC, nsz, in_tile, in_fn, cb_s, coff=col_s % 128)

        hf = lambda kc: h[:, kc, :]
        rope_pair(wa_d, 8, h, hf, WA_MQR, WA_MQS, 128, R_MQ, 0)
        for j in range(3):
            cm.lin(wa_d[WA_MQP // 128 + j], 8, 128, h, hf, copy_out(R_MQ + 128 + j * 128))
        rope_pair(wa_d, 8, h, hf, WA_MKR, WA_MKS, 128, R_MK, 0)
        for j in range(3):
            cm.lin(wa_d[WA_MKP // 128 + j], 8, 128, h, hf, copy_out(R_MK + 128 + j * 128))
        for j in range(4):
            cm.lin(wa_d[WA_MV // 128 + j], 8, 128, h, hf, copy_out(R_MV + j * 128))
        for j in range(12):
            cm.lin(wa_d[WA_SQ // 128 + j], 8, 128, h, hf, copy_out(R_SQ + j * 128))
        for j in range(2):
            def cb(ps, j=j):
                _act(c, cq[:, j, :], ps[:, :], AF.Copy, [ps], [cq])
            cm.lin(wa_d[WA_CQ // 128 + j], 8, 128, h, hf, cb)
        cm.rstd(cq, lambda kc: cq[:, kc, :], 2, 256, R)
        for j in range(2):
            c.op("dve", lambda j=j: nc.vector.tensor_tensor(out=cq[:, j, :], in0=cq[:, j, :], in1=R[:, 0:TT], op=ALU.mult),
                 reads=[cq, R], writes=[cq])
            _act(c, cqn[:, j, :], cq[:, j, :], AF.Identity, [cq, qn], [cqn], scale=qn[:, j:j + 1])
        qf = lambda kc: cqn[:, kc, :]
        for j in range(4):
            cm.lin(wuq_d[j], 2, 128, cqn, qf, copy_out(R_CQN + j * 128))
        for j in range(2):
            rope_pair(wuq_d, 2, cqn, qf, 512 + j * 128, 768 + j * 128, 128, R_CQR + j * 128, 2)
        def cbkv(ps):
            _act(c, cq[:, 0, :], ps[:, :], AF.Copy, [ps], [cq])
        cm.lin(wa_d[WA_CKV // 128], 8, 128, h, hf, cbkv)
        cm.rstd(cq, lambda kc: cq[:, kc, :], 1, 128, R)
        c.op("dve", lambda: nc.vector.tensor_tensor(out=cq[:, 0, :], in0=cq[:, 0, :], in1=R[:, 0:TT], op=ALU.mult),
             reads=[cq, R], writes=[cq])
        _act(c, cqn[:, 0, :], cq[:, 0, :], AF.Identity, [cq, kvn], [cqn], scale=kvn[:, 0:1])
        for j in range(8):
            cm.lin(wukv_d[j], 1, 128, cqn, qf, copy_out(R_CKN + j * 128))
        rope_pair(wa_d, 8, h, hf, WA_KRR, WA_KRS, 32, R_KR, 2)
    c.wait_all("sp", [x1_d, hm_d, x, h] + ob)
    c.close()
    return nc


NKT = SEQ // 128
NQT = SEQ // TT


def build_B():
    nc = bass.Bass("TRN2", target_bir_lowering=False)
    c = Ctx(nc)
    di = lambda n, s, dt=BF16: c.dram(n, s, dt, "ExternalInput")
    q_d = [di("q%d" % i, [dk, SEQ]) for i, dk in enumerate((64, 64, 96))]
    k_d = [di("k%d" % i, [dk, SEQ]) for i, dk in enumerate((64, 64, 96))]
    v_d = [di("v%d" % i, [64, SEQ]) for i in range(3)]
    eblk_d = di("eblk", [64, SEQ])
    ident_d = di("ident", [128, 128])
    cm_d = di("cmask", [2, 4, 128, TT])
    tri_d = di("tri", [128, 128])
    y_d = c.dram("yT", [3, 64, SEQ], F32, "ExternalOutput")

    qa = c.dsem(c.sb("qa", [128, SEQ], BF16))
    ka = c.dsem(c.sb("ka", [128, SEQ], BF16))
    vt = c.dsem(c.sb("vt", [64, SEQ], BF16))
    vaug = c.sb("vaug", [128, NKT, 65], BF16)
    ident = c.dsem(c.sb("ident", [128, 128], BF16))
    cmask = c.dsem(c.sb("cmask", [128, 8, TT], BF16))
    tri = c.dsem(c.sb("tri", [128, 128], BF16))
    ones8 = c.sb("ones8", [128, 128], BF16)
    onesf = c.sb("onesf", [128, 64], F32)
    c.dma("sp", ident[:], ident_d[:, :], writes=[ident])
    c.dma("sp", cmask[:], cm_d.t.ap().rearrange("a m p t -> p (a m) t"), writes=[cmask])
    c.dma("sp", tri[:], tri_d[:, :], writes=[tri])
    c.op("dve", lambda: nc.vector.memset(ones8[:], -8.0), writes=[ones8])
    c.op("dve", lambda: nc.vector.memset(onesf[:], 1.0), writes=[onesf])
    c.op("dve", lambda: nc.vector.memset(vaug[:, :, 64:65], 1.0), writes=[vaug])

    pss = [c.ps("pss%d" % i) for i in range(3)]
    pse = [c.ps("pse%d" % i) for i in range(2)]
    po = c.ps("po")
    pb = c.ps("pb")
    ptr = c.ps("ptr", (128, 1024), BF16)
    pt = [c.sb("pt%d" % i, [128, TT], BF16) for i in range(3)]
    ef = [c.sb("ef%d" % i, [128, TT], F32) for i in range(2)]
    spb = [c.sb("spb%d" % i, [128, TT], BF16) for i in range(3)]
    rsum = c.sb("rsum", [128, TT], F32)
    rsb = [c.sb("rsb%d" % i, [128, TT], BF16) for i in range(2)]
    drow = c.sb("drow", [128, TT], F32)
    dbc = c.sb("dbc", [64, TT], F32)
    yo = [c.dsem(c.sb("yo%d" % i, [64, TT], F32)) for i in range(2)]
    cnt = [0]

    def load(kind):
        dk = (64, 64, 96)[kind]
        c.dma("sp", qa[0:dk, :], q_d[kind][:, :], writes=[qa])
        c.dma("sp", ka[0:dk, :], k_d[kind][:, :], writes=[ka])
        c.dma("sp", vt[:, :], v_d[kind][:, :], writes=[vt])
        if kind == 0:
            c.dma("sp", ka[64:128, :], eblk_d[:, :], writes=[ka])
        for kt in range(NKT):
            c.op("pe", lambda kt=kt: nc.tensor.transpose(ptr[:, 0:64], vt[0:64, kt * 128:(kt + 1) * 128], ident[0:64, 0:64]),
                 reads=[vt, ident], writes=[ptr])
            c.op("dve", lambda kt=kt: nc.vector.tensor_copy(out=vaug[:, kt, 0:64], in_=ptr[:, 0:64]), reads=[ptr], writes=[vaug])

    def moba_gate():
        kms = c.sb("kms", [64, 64], F32)
        kmb = c.sb("kmb", [64, 64], F32)
        qf32 = c.sb("qf32", [64, 128], F32)
        gm = c.sb("gm", [128, 64], F32)
        mx = c.sb("mx", [128, 8], F32)
        mbp = c.sb("mbp", [128, 128], BF16)
        c.op("dve", lambda: nc.vector.tensor_reduce(out=kms[:, :], in_=ka[0:64, :].rearrange("p (b j) -> p b j", j=256),
                                                    op=ALU.add, axis=AX.X), reads=[ka], writes=[kms])
        c.op("dve", lambda: nc.vector.tensor_scalar(out=kmb[:, :], in0=kms[:, :], scalar1=1.0 / 256, scalar2=None, op0=ALU.mult),
             reads=[kms], writes=[kmb])
        c.op("dve", lambda: nc.vector.memset(mbp[:, 0:64], 0.0), writes=[mbp])
        for tt in range(NKT):
            qb = tt // 2
            ps = pss[tt % 3]
            c.op("dve", lambda tt=tt: nc.vector.tensor_copy(out=qf32[:, :], in_=qa[0:64, tt * 128:(tt + 1) * 128]), reads=[qa], writes=[qf32])
            c.op("pe", lambda tt=tt, ps=ps: nc.tensor.matmul(ps[:, 0:64], lhsT=qf32[:, :], rhs=kmb[:, :],
                                                             start=True, stop=True), reads=[qf32, kmb], writes=[ps])
            c.op("dve", lambda: nc.vector.memset(gm[:, :], -1e30), writes=[gm])
            if qb > 0:
                c.op("dve", lambda ps=ps, qb=qb: nc.vector.tensor_copy(out=gm[:, 0:qb], in_=ps[:, 0:qb]), reads=[ps], writes=[gm])
            c.op("dve", lambda: nc.vector.max(out=mx[:, :], in_=gm[:, :]), reads=[gm], writes=[mx])
            c.op("dve", lambda: nc.vector.tensor_scalar(out=mx[:, 2:3], in0=mx[:, 2:3], scalar1=-1e29, scalar2=None, op0=ALU.max),
                 reads=[mx], writes=[mx])
            c.op("dve", lambda: nc.vector.tensor_scalar(out=gm[:, :], in0=gm[:, :], scalar1=mx[:, 2:3], scalar2=None, op0=ALU.is_ge),
                 reads=[gm, mx], writes=[gm])
            c.op("dve", lambda qb=qb: nc.vector.memset(gm[:, qb:qb + 1], 1.0), writes=[gm])
            c.op("dve", lambda: nc.vector.tensor_scalar(out=mbp[:, 64:128], in0=gm[:, :], scalar1=1.0, scalar2=-NEG,
                                                        op0=ALU.subtract, op1=ALU.mult), reads=[gm], writes=[mbp])
            pq = pse[tt % 2]
            c.op("pe", lambda pq=pq: nc.tensor.matmul(pq[:, 0:128], lhsT=mbp[:, :], rhs=ident[:, :], start=True, stop=True),
                 reads=[mbp, ident], writes=[pq])
            c.op("dve", lambda pq=pq, tt=tt: nc.vector.tensor_copy(out=qa[64:128, tt * 128:(tt + 1) * 128], in_=pq[64:128, 0:128]),
                 reads=[pq], writes=[qa])

    def attn(kind):
        dk = (128, 64, 96)[kind]
        scale = (64 ** -0.5, 64 ** -0.5, 96 ** -0.5)[kind]
        sb_mode = (kind == 1)
        mrow = 4 if sb_mode else 0
        nv = 64 if sb_mode else 65
        steps = []
        for qt in range(NQT):
            nk = 4 * qt + 4
            order = list(range(nk - 1, -1, -1)) if sb_mode else list(range(nk))
            for i, kt in enumerate(order):
                steps.append((qt, i, kt, nk))
        N = len(steps)

        def zmm(ps, qt, kt, last_stop):
            qs = slice(qt * TT, (qt + 1) * TT); ks = slice(kt * 128, (kt + 1) * 128)
            diag = kt >= 4 * qt
            c.op("pe", lambda: nc.tensor.matmul(ps[:, :], lhsT=ka[0:dk, ks], rhs=qa[0:dk, qs], start=True, stop=(last_stop and not diag)),
                 reads=[ka, qa], writes=[ps])
            if diag:
                m = kt - 4 * qt
                c.op("pe", lambda: nc.tensor.matmul(ps[:, :], lhsT=ident[:, :], rhs=cmask[:, mrow + m, :], start=False, stop=last_stop),
                     reads=[ident, cmask], writes=[ps])

        def s1a(n):
            qt, i, kt, nk = steps[n]
            ps = pss[n % 3]
            zmm(ps, qt, kt, not sb_mode)
            if not sb_mode:
                _act(c, pt[n % 3][:], ps[:, :], AF.Exp, [ps], [pt[n % 3]], scale=scale)
            else:
                e_t = ef[n % 2]; s_t = spb[n % 3]
                _act(c, e_t[:], ps[:, :], AF.Exp, [ps], [e_t], scale=scale)
                _act(c, s_t[:], e_t[:], AF.Ln, [e_t], [s_t], bias=1.0)

        def s1b(n):
            qt, i, kt, nk = steps[n]
            if not sb_mode or i == nk - 1:
                return
            s_t = spb[n % 3]; rb = rsb[(n + 1) % 2]
            if i == 0:
                c.op("pool", lambda: nc.gpsimd.tensor_copy(out=rsum[:], in_=s_t[:]), reads=[s_t], writes=[rsum])
            else:
                c.op("pool", lambda: nc.gpsimd.tensor_tensor(out=rsum[:], in0=rsum[:], in1=s_t[:], op=ALU.add),
                     reads=[s_t, rsum], writes=[rsum])
            c.op("dve", lambda: nc.vector.tensor_copy(out=rb[:], in_=rsum[:]), reads=[rsum], writes=[rb])

        def s2(n):
            qt, i, kt, nk = steps[n]
            pe_ = pss[n % 3]; s_t = spb[n % 3]; rb = rsb[n % 2]
            c.op("pe", lambda: nc.tensor.matmul(pe_[:, :], lhsT=tri[:, :], rhs=s_t[:], start=False, stop=(i == 0)),
                 reads=[tri, s_t], writes=[pe_])
            if i > 0:
                c.op("pe", lambda: nc.tensor.matmul(pe_[:, :], lhsT=ones8[:, :], rhs=rb[:], start=False, stop=True),
                     reads=[ones8, rb], writes=[pe_])
            _act(c, pt[n % 3][:], pe_[:, :], AF.Exp, [pe_], [pt[n % 3]], scale=scale)

        def s3(n):
            qt, i, kt, nk = steps[n]
            p_t = pt[n % 3]
            c.op("pe", lambda: nc.tensor.matmul(po[0:nv, :], lhsT=vaug[:, kt, 0:nv], rhs=p_t[:], start=(i == 0), stop=(i == nk - 1)),
                 reads=[vaug, p_t], writes=[po])
            if i == nk - 1:
                finalize(qt)

        def finalize(qt):
            qs = slice(qt * TT, (qt + 1) * TT)
            o = yo[qt % 2]
            if sb_mode:
                _act(c, o[:, :], po[0:64, :], AF.Copy, [po], [o])
            else:
                _act(c, drow[64:65, :], po[64:65, :], AF.Copy, [po], [drow])
                c.op("dve", lambda: nc.vector.reciprocal(out=drow[64:65, :], in_=drow[64:65, :]), reads=[drow], writes=[drow])
                c.op("pe", lambda: nc.tensor.matmul(pb[0:64, :], lhsT=onesf[64:65, 0:64], rhs=drow[64:65, :], start=True, stop=True),
                     reads=[onesf, drow], writes=[pb])
                _act(c, dbc[:, :], pb[0:64, :], AF.Copy, [pb], [dbc])
                c.op("dve", lambda: nc.vector.tensor_tensor(out=o[:, :], in0=po[0:64, :], in1=dbc[:, :], op=ALU.mult),
                     reads=[po, dbc], writes=[o])
            c.dma("sp", y_d[kind, :, qs], o[:, :], reads=[o], sem_tile=o)

        if sb_mode:
            for n in range(N + 2):
                if n < N:
                    s1a(n)
                if 0 <= n - 1 < N:
                    s2(n - 1)
                if 0 <= n - 2 < N:
                    s3(n - 2)
                if n < N:
                    s1b(n)
        else:
            for n in range(N + 1):
                if n < N:
                    s1a(n)
                if 0 <= n - 1 < N:
                    s3(n - 1)

    for kind in range(3):
        load(kind)
        if kind == 0:
            moba_gate()
        attn(kind)
    c.wait_all("sp", yo)
    c.close()
    return nc


def build_C():
    nc = bass.Bass("TRN2", target_bir_lowering=False)
    c = Ctx(nc)
    di = lambda n, s, dt=F32: c.dram(n, s, dt, "ExternalInput")
    xT_d = di("xT", [D, TPC]); hm_d = di("hmT", [D, TPC], BF16); yin_d = di("yin", [1536, TPC])
    cvec_d = di("cvec", [128, 8]); adaw_d = di("adaw", [D, 9216])
    adab_d = di("adab", [128, 72]); npre_d = di("npre", [128, 24]); npost_d = di("npost", [128, 24])
    wg_d = di("wg", [FFC, 128, 1024]); wu_d = di("wu", [FFC, 128, 1024]); wd_d = di("wd", [8, 128, DFF])
    wgate_d = di("wgate", [24, 128, 1024]); bg_d = di("bg", [128, 24])
    wbr_d = di("wbr", [24, 128, 512]); wout_d = di("wout", [8, 128, 1024])
    x3_d = c.dsem(c.dram("x3T", [D, TPC], F32, "ExternalOutput"))
    cm = Common(c)
    cm.setup_mod(cvec_d, adaw_d, adab_d, npre_d, npost_d)
    bg = c.dsem(c.sb("bg", [128, 24], F32))
    c.dma("sp", bg[:], bg_d[:, :], writes=[bg])
    x = c.dsem(c.sb("x", [128, 8, TT], F32))
    h = c.dsem(c.sb("h", [128, 8, TT], BF16))
    yin = c.dsem(c.sb("yin", [128, 12, TT], BF16))
    act = c.sb("act", [128, FFC, TT], BF16)
    y = c.sb("y", [128, 8, TT], F32)
    tmp = c.sb("tmp", [128, 8, TT], F32)
    R = c.sb("R", [128, TT], F32)
    mg = c.sb("mg", [128, 8, TT], BF16)
    bs = c.sb("bs", [128, TT], F32)
    sg = c.sb("sg", [128, TT], F32)
    macc = c.sb("macc", [128, TT], F32)
    for t in range(TPC // TT):
        tsl = slice(t * TT, (t + 1) * TT)
        c.dma("sp", x[:], xT_d[:, tsl].rearrange("(kc p) t -> p kc t", p=128), writes=[x])
        c.dma("sp", h[:], hm_d[:, tsl].rearrange("(kc p) t -> p kc t", p=128), writes=[h])
        c.dma("pool", yin[:], yin_d[:, tsl].rearrange("(kc p) t -> p kc t", p=128), writes=[yin])
        for dc in range(8):
            for n in range(3):
                def cb_b(ps):
                    _act(c, bs[:], ps[:, :], AF.Copy, [ps], [bs])
                cm.lin(wbr_d[n * 8 + dc], 4, 128, yin, lambda kc, n=n: yin[:, n * 4 + kc, :], cb_b)
                def cb_g(ps, n=n, dc=dc):
                    _act(c, sg[:], ps[:, :], AF.Sigmoid, [ps, bg], [sg], bias=bg[:, n * 8 + dc:n * 8 + dc + 1])
                cm.lin(wgate_d[n * 8 + dc], 8, 128, h, lambda kc: h[:, kc, :], cb_g)
                if n == 0:
                    c.op("dve", lambda: nc.vector.tensor_tensor(out=macc[:], in0=bs[:], in1=sg[:], op=ALU.mult),
                         reads=[bs, sg], writes=[macc])
                else:
                    c.op("dve", lambda: nc.vector.tensor_tensor(out=bs[:], in0=bs[:], in1=sg[:], op=ALU.mult),
                         reads=[bs, sg], writes=[bs])
                    c.op("dve", lambda: nc.vector.tensor_tensor(out=macc[:], in0=macc[:], in1=bs[:], op=ALU.add),
                         reads=[bs, macc], writes=[macc])
            c.op("dve", lambda dc=dc: nc.vector.tensor_copy(out=mg[:, dc, :], in_=macc[:]), reads=[macc], writes=[mg])
        for dc in range(8):
            def cb_o(ps, dc=dc):
                _act(c, y[:, dc, :], ps[:, :], AF.Copy, [ps], [y])
            cm.lin(wout_d[dc], 8, 128, mg, lambda kc: mg[:, kc, :], cb_o)
        cm.rstd(y, lambda kc: y[:, kc, :], 8, D, R)
        cm.residual(1, x, y, R, tmp)
        cm.ffn(2, wg_d, wu_d, wd_d, x, h, act, y, tmp, R)
        c.dma("sp", x3_d[:, tsl].rearrange("(kc p) t -> p kc t", p=128), x[:], reads=[x], writes=[x3_d])
    c.wait_all("sp", [x3_d, x])
    c.close()
    return nc


def _vec8(v):
    return np.ascontiguousarray(np.asarray(v, np.float32).reshape(-1, 8, 128).transpose(2, 0, 1).reshape(128, -1))


def _rope_consts():
    import ml_dtypes
    pos = np.arange(SEQ, dtype=np.float32)
    def tabs(n_rot, reps):
        inv = np.power(np.float32(THETA), -np.arange(0, n_rot, 2, dtype=np.float32) / np.float32(n_rot)).astype(np.float32)
        ang = (pos[:, None] * inv[None, :]).astype(np.float32)
        co, si = np.cos(ang).astype(np.float32), np.sin(ang).astype(np.float32)
        cos_rows = np.concatenate([co, co], axis=1).T
        sin_rows = np.concatenate([-si, si], axis=1).T
        return np.tile(cos_rows, (reps, 1)), np.tile(sin_rows, (reps, 1))
    cp, sp_ = tabs(16, 8)
    cmm, sm = tabs(32, 4)
    return np.ascontiguousarray(np.stack([cp, sp_, cmm, sm]).astype(np.float32))


def _consts_B():
    import ml_dtypes
    bf = ml_dtypes.bfloat16
    eblk = (np.arange(SEQ)[None, :] // 256 == np.arange(64)[:, None]).astype(np.float32).astype(bf)
    ident = np.eye(128, dtype=np.float32).astype(bf)
    k = np.arange(128)[:, None]; t = np.arange(TT)[None, :]
    cmk = np.zeros((2, 4, 128, TT), np.float32)
    for m in range(4):
        cmk[0, m] = np.where(128 * m + k <= t, 0.0, NEG)
        cmk[1, m] = np.where(128 * m + k < t, 0.0, NEG)
    j = np.arange(128)[:, None]; s = np.arange(128)[None, :]
    tri = np.where(j >= s, -8.0, 0.0).astype(np.float32).astype(bf)
    return dict(eblk=eblk, ident=ident, cmask=cmk.astype(bf), tri=tri)


def _perm_wa(w_in):
    h = np.arange(8)[:, None]
    def cols(base, js):
        return (base + h * 64 + np.asarray(js)[None, :]).reshape(-1)
    r = np.arange(16); s = (r + 8) % 16; p = np.arange(16, 64)
    idx = []
    for base in (0, 512):
        idx += [cols(base, r), cols(base, s), cols(base, p)]
    idx += [np.arange(1024, 1536), np.arange(1536, 3072), np.arange(3072, 3328), np.arange(3328, 3456)]
    kr = 3456 + np.arange(32)
    idx += [kr, 3456 + (np.arange(32) + 16) % 32]
    idx = np.concatenate(idx)
    assert idx.shape[0] == NWA
    return np.ascontiguousarray(w_in[:, idx])


def _perm_wuq(w):
    h = np.arange(8)[:, None]
    n = (h * 96 + np.arange(64)[None, :]).reshape(-1)
    r = (h * 96 + 64 + np.arange(32)[None, :]).reshape(-1)
    s = (h * 96 + 64 + ((np.arange(32) + 16) % 32)[None, :]).reshape(-1)
    return np.ascontiguousarray(w[:, np.concatenate([n, r, s])])


def _perm_wukv(w):
    h = np.arange(8)[:, None]
    kn = (h * 128 + np.arange(64)[None, :]).reshape(-1)
    v = (h * 128 + 64 + np.arange(64)[None, :]).reshape(-1)
    return np.ascontiguousarray(w[:, np.concatenate([kn, v])])


def _pre(w):
    w = np.asarray(w, np.float32)
    K, N = w.shape
    if N % 128:
        w = np.concatenate([w, np.zeros((K, 128 - N % 128), np.float32)], axis=1)
        N = w.shape[1]
    KC, NC_ = K // 128, N // 128
    return np.ascontiguousarray(w.reshape(KC, 128, NC_, 128).transpose(2, 1, 0, 3).reshape(NC_, 128, KC * 128))


_NC = {}


def _get(name, fn):
    if name not in _NC:
        _NC[name] = fn()
    return _NC[name]


def kernel(x, c, ada_w, ada_b, norm_pre, norm_post, ffn_w_gate, ffn_w_up, ffn_w_down, mix_w_in, mix_b_gate,
           mla_q_norm, mla_w_uq, mla_kv_norm, mla_w_ukv, mix_w_branch, mix_w_out):
    f = lambda a: np.ascontiguousarray(np.asarray(a, np.float32))
    x = f(x); cores = list(range(NCORE))
    tabs = _rope_consts()
    cb = _consts_B()
    cvec = _vec8(f(c)[0])
    xT = [np.ascontiguousarray(x[0, i * TPC:(i + 1) * TPC, :].T) for i in cores]
    for l in range(2):
        common = dict(cvec=cvec, adaw=f(ada_w[l]), adab=_vec8(f(ada_b[l])), npre=_vec8(f(norm_pre[l])), npost=_vec8(f(norm_post[l])))
        wa = _pre(_perm_wa(f(mix_w_in[l]))); wuq = _pre(_perm_wuq(f(mla_w_uq[l]))); wukv = _pre(_perm_wukv(f(mla_w_ukv[l])))
        qn = np.ascontiguousarray(f(mla_q_norm[l]).reshape(2, 128).T); kvn = np.ascontiguousarray(f(mla_kv_norm[l]).reshape(1, 128).T)
        ins = [dict(common, xT=xT[i], wg=_pre(ffn_w_gate[l, 0]), wu=_pre(ffn_w_up[l, 0]), wd=_pre(ffn_w_down[l, 0]), wa=wa, wuq=wuq, wukv=wukv,
                    qn=qn, kvn=kvn, tab=np.ascontiguousarray(tabs[:, :, i * TPC:(i + 1) * TPC])) for i in cores]
        ra = run_bass_kernel_spmd(_get("A", build_A), ins, core_ids=cores).results
        oa = np.concatenate([np.asarray(r["oa"]) for r in ra], axis=1)
        insb = []
        for hd in cores:
            mq = np.concatenate([oa[R_MQ + hd * 16:R_MQ + hd * 16 + 16], oa[R_MQ + 128 + hd * 48:R_MQ + 128 + hd * 48 + 48]], 0)
            mk = np.concatenate([oa[R_MK + hd * 16:R_MK + hd * 16 + 16], oa[R_MK + 128 + hd * 48:R_MK + 128 + hd * 48 + 48]], 0)
            mv = oa[R_MV + hd * 64:R_MV + hd * 64 + 64]
            sq = oa[R_SQ + hd * 64:R_SQ + hd * 64 + 64]; sk = oa[R_SK + hd * 64:R_SK + hd * 64 + 64]; sv = oa[R_SV + hd * 64:R_SV + hd * 64 + 64]
            cq_ = np.concatenate([oa[R_CQN + hd * 64:R_CQN + hd * 64 + 64], oa[R_CQR + hd * 32:R_CQR + hd * 32 + 32]], 0)
            ck_ = np.concatenate([oa[R_CKN + hd * 64:R_CKN + hd * 64 + 64], oa[R_KR:R_KR + 32]], 0)
            cv_ = oa[R_CV + hd * 64:R_CV + hd * 64 + 64]
            d = dict(cb)
            for i, (q_, k_, v_) in enumerate(((mq, mk, mv), (sq, sk, sv), (cq_, ck_, cv_))):
                d["q%d" % i] = np.ascontiguousarray(q_); d["k%d" % i] = np.ascontiguousarray(k_); d["v%d" % i] = np.ascontiguousarray(v_)
            insb.append(d)
        rb = run_bass_kernel_spmd(_get("B", build_B), insb, core_ids=cores).results
        yall = np.stack([np.asarray(r["yT"]) for r in rb], axis=1)
        yall = yall.reshape(3 * 512, SEQ)
        insc = [dict(common, xT=np.asarray(ra[i]["x1T"]), hmT=np.asarray(ra[i]["hmT"]), yin=np.ascontiguousarray(yall[:, i * TPC:(i + 1) * TPC]),
                     wg=_pre(ffn_w_gate[l, 1]), wu=_pre(ffn_w_up[l, 1]), wd=_pre(ffn_w_down[l, 1]),
                     wgate=_pre(f(mix_w_in[l])[:, 3488:]), bg=_vec8(f(mix_b_gate[l])),
                     wbr=np.concatenate([_pre(f(mix_w_branch[l])[n]) for n in range(3)], axis=0), wout=_pre(mix_w_out[l])) for i in cores]
        rc = run_bass_kernel_spmd(_get("C", build_C), insc, core_ids=cores).results
        xT = [np.asarray(r["x3T"]) for r in rc]
    out = np.concatenate([t.T for t in xT], axis=0)[None]
    return np.ascontiguousarray(out.astype(np.float32))
```
